# Optimizing a Trainium2 kernel written in Bass

```python
import math
import jax, jax.numpy as jnp
from jax import lax
import numpy as np

D_MODEL = 2048
BATCH = 8
SEQ = 2048
DEPTH = 2

GRID_W = 64
CTX_LEN = 256
HY_W = D_MODEL // 4
ML_W = 3 * D_MODEL // 8
RT_W = 3 * D_MODEL // 8
HY_EMB = 33
HY_BANDS = (HY_EMB - 1) // 2
HY_FF = 64
HY_DECAY_TARGET = 1e-2
HY_SHORT_PCT = 0.3
HY_LONG_PCT = 1.5
ML_HEADS = 4
ML_DH = ML_W // ML_HEADS
ML_CHUNK = 128
RT_HEADS = 4
RT_DV = RT_W // RT_HEADS
RT_DK = RT_DV // 2
RT_CHUNK = 128
ROPE_BASE = 10000.0
D_FF = 4 * D_MODEL
N_MOD = 6
EPS = 1e-6
HY_COLS = 3 * HY_W
ML_COLS = 4 * ML_W
MLG_COLS = 4 * ML_HEADS
RT_COLS = 2 * RT_HEADS * RT_DK + 2 * RT_W
N_IN = HY_COLS + ML_COLS + MLG_COLS + RT_COLS
SPLITS = [HY_COLS, HY_COLS + ML_COLS, HY_COLS + ML_COLS + MLG_COLS]

kernel_name = "hybrid_hyena_mlstm_retention_dit_prefix"


def rms_norm(x, g):
    xf = x.astype(jnp.float32)
    y = xf * lax.rsqrt(jnp.mean(xf * xf, axis=-1, keepdims=True) + EPS)
    return (y * g.astype(jnp.float32)).astype(x.dtype)


def _head_rms(h):
    return h * lax.rsqrt(jnp.mean(h * h, axis=-1, keepdims=True) + EPS)


def modulate(h, shift, scale):
    return h * (1.0 + scale) + shift


def short_conv3(u, w):
    up = jnp.pad(u, ((0, 0), (1, 1), (0, 0)))
    return up[:, :-2] * w[0] + up[:, 1:-1] * w[1] + up[:, 2:] * w[2]


def _heads(a, n_heads, dh):
    B, L, _ = a.shape
    return a.reshape(B, L, n_heads, dh).transpose(0, 2, 1, 3).astype(jnp.float32)


def _chunks(a, T):
    B, H, L = a.shape[:3]
    return jnp.moveaxis(a.reshape(B, H, L // T, T, *a.shape[3:]), 2, 0)


def _unchunk(a):
    nc, B, H, T = a.shape[:4]
    return jnp.moveaxis(a, 0, 2).reshape(B, H, nc * T, *a.shape[4:])


def hyena_filter(L, w1, b1, w2, b2, w3, freq):
    f32 = jnp.float32
    t = jnp.linspace(0.0, 1.0, L, dtype=f32)[:, None]
    w = (2.0 * math.pi / L) * jnp.arange(L, dtype=f32)[:, None]
    bands = jnp.linspace(1e-4, HY_BANDS - 1, HY_BANDS, dtype=f32)[None, :]
    z = jnp.concatenate([t, jnp.cos(bands * w), -jnp.sin(bands * w)], axis=-1)
    fr = freq.astype(f32)
    hdn = jnp.sin(fr * (z @ w1.astype(f32) + b1.astype(f32)))
    hdn = jnp.sin(fr * (hdn @ w2.astype(f32) + b2.astype(f32)))
    h = (hdn @ w3.astype(f32)).reshape(L, 2, HY_W)
    deltas = jnp.abs(jnp.linspace(math.log(HY_DECAY_TARGET) / HY_LONG_PCT,
                                  math.log(HY_DECAY_TARGET) / HY_SHORT_PCT, HY_W, dtype=f32))
    h = h * jnp.exp(-t * deltas)[:, None, :]
    hf, hb = h[:, 0], h[:, 1]
    filt = jnp.concatenate([hf, jnp.zeros((1, HY_W), f32), hb[:0:-1]], axis=0)
    return filt / jnp.sum(jnp.abs(filt), axis=0, keepdims=True)


def fft_long_conv(u, filt, bias):
    L = u.shape[1]
    uf = jnp.fft.rfft(u.astype(jnp.float32), n=2 * L, axis=1)
    ff = jnp.fft.rfft(filt, n=2 * L, axis=0)
    y = jnp.fft.irfft(uf * ff[None], n=2 * L, axis=1)[:, :L]
    return (y + u.astype(jnp.float32) * bias.astype(jnp.float32)).astype(u.dtype)


def hyena_mixer(p, conv_w, filt, bias):
    u = short_conv3(p, conv_w)
    x0, x1, v = jnp.split(u, 3, axis=-1)
    return fft_long_conv(v * x1, filt, bias) * x0


def mlstm_scan(q, k, v, log_i, log_f, state):
    T = ML_CHUNK
    causal = jnp.tril(jnp.ones((T, T), dtype=bool))

    def step(carry, xs):
        C, n, m = carry
        qc, kc, vc, ic, fc = xs
        b = jnp.cumsum(fc, axis=-1)
        dmat = jnp.where(causal, b[..., :, None] - b[..., None, :] + ic[..., None, :], -jnp.inf)
        m_inter = b + m[..., None]
        m_t = jnp.maximum(m_inter, jnp.max(dmat, axis=-1))
        s = jnp.einsum("bhtd,bhsd->bhts", qc, kc) * jnp.exp(dmat - m_t[..., None])
        w_inter = jnp.exp(m_inter - m_t)
        num = jnp.einsum("bhts,bhse->bhte", s, vc) + w_inter[..., None] * jnp.einsum("bhtd,bhde->bhte", qc, C)
        den = jnp.sum(s, axis=-1) + w_inter * jnp.einsum("bhtd,bhd->bht", qc, n)
        h = num / jnp.maximum(jnp.abs(den), jnp.exp(-m_t))[..., None]
        b_end = b[..., -1]
        g = b_end[..., None] - b + ic
        m_new = jnp.maximum(b_end + m, jnp.max(g, axis=-1))
        wk = jnp.exp(g - m_new[..., None])
        carry_decay = jnp.exp(b_end + m - m_new)
        C_new = carry_decay[..., None, None] * C + jnp.einsum("bhsd,bhse->bhde", kc * wk[..., None], vc)
        n_new = carry_decay[..., None] * n + jnp.einsum("bhsd,bhs->bhd", kc, wk)
        return (C_new, n_new, m_new), h

    xs = (_chunks(q, T), _chunks(k, T), _chunks(v, T), _chunks(log_i, T), _chunks(log_f, T))
    state, hs = lax.scan(step, state, xs)
    return _unchunk(hs), state


def _mlstm_prep(p, graw, conv_w, gate_b):
    qk, v, o = jnp.split(p, [2 * ML_W, 3 * ML_W], axis=-1)
    qk = jax.nn.silu(short_conv3(qk, conv_w))
    q, k = jnp.split(qk, 2, axis=-1)
    q = _heads(q, ML_HEADS, ML_DH)
    k = _heads(k, ML_HEADS, ML_DH) * (ML_DH ** -0.5)
    v = _heads(v, ML_HEADS, ML_DH)
    B, L, _ = p.shape
    gates = (graw + gate_b).astype(jnp.float32).reshape(B, L, 4, ML_HEADS).transpose(2, 0, 3, 1)
    g_fwd = (gates[0], jax.nn.log_sigmoid(gates[1]))
    g_bwd = (gates[2], jax.nn.log_sigmoid(gates[3]))
    return q, k, v, o, g_fwd, g_bwd


def _mlstm_bidir(q, k, v, g_fwd, g_bwd, init_f, init_b):
    h_f, st_f = mlstm_scan(q, k, v, g_fwd[0], g_fwd[1], init_f)
    fl = lambda a: jnp.flip(a, axis=2)
    h_b, st_b = mlstm_scan(fl(q), fl(k), fl(v), fl(g_bwd[0]), fl(g_bwd[1]), init_b)
    return h_f + fl(h_b), st_f, st_b


def _mlstm_out(h, o, g):
    B, H, L, dh = h.shape
    hn = _head_rms(h) * g.astype(jnp.float32).reshape(H, 1, dh)
    return (hn.transpose(0, 2, 1, 3).reshape(B, L, H * dh) * jax.nn.sigmoid(o.astype(jnp.float32))).astype(o.dtype)


def mlstm_group(pc, gc, px, gx, conv_w, gate_b, norm_g):
    B = pc.shape[0]
    f32 = jnp.float32
    zero = (jnp.zeros((B, ML_HEADS, ML_DH, ML_DH), f32), jnp.zeros((B, ML_HEADS, ML_DH), f32),
            jnp.zeros((B, ML_HEADS), f32))
    qc, kc, vc, oc, gfc, gbc = _mlstm_prep(pc, gc, conv_w, gate_b)
    hc, st_f, st_b = _mlstm_bidir(qc, kc, vc, gfc, gbc, zero, zero)
    qx, kx, vx, ox, gfx, gbx = _mlstm_prep(px, gx, conv_w, gate_b)
    hx, _, _ = _mlstm_bidir(qx, kx, vx, gfx, gbx, st_f, st_b)
    return _mlstm_out(hc, oc, norm_g), _mlstm_out(hx, ox, norm_g)


def axial_rope(L):
    rows = L // GRID_W
    r = jnp.repeat(jnp.arange(rows, dtype=jnp.float32), GRID_W)
    col = jnp.tile(jnp.arange(GRID_W, dtype=jnp.float32), rows)
    nf = RT_DK // 4
    inv = ROPE_BASE ** (-jnp.arange(nf, dtype=jnp.float32) / nf)
    ang = jnp.concatenate([r[:, None] * inv, col[:, None] * inv], axis=-1)
    return jnp.cos(ang), jnp.sin(ang)


def apply_rope(a, cos, sin):
    a1, a2 = jnp.split(a, 2, axis=-1)
    return jnp.concatenate([a1 * cos - a2 * sin, a1 * sin + a2 * cos], axis=-1)


def retention_scan(q, k, v, log_g, S):
    T = RT_CHUNK
    idx = jnp.arange(T, dtype=jnp.float32)
    rel = idx[:, None] - idx[None, :]
    dmask = jnp.where(rel >= 0, jnp.exp(log_g[:, None, None] * jnp.maximum(rel, 0.0)), 0.0)
    q_decay = jnp.exp(log_g[:, None] * (idx + 1.0))[..., None]
    k_decay = jnp.exp(log_g[:, None] * (T - 1.0 - idx))[..., None]
    c_decay = jnp.exp(log_g * T)[:, None, None]

    def step(S, xs):
        qc, kc, vc = xs
        s = jnp.einsum("bhtd,bhsd->bhts", qc, kc) * dmask
        o = jnp.einsum("bhts,bhse->bhte", s, vc) + q_decay * jnp.einsum("bhtd,bhde->bhte", qc, S)
        S = c_decay * S + jnp.einsum("bhsd,bhse->bhde", kc * k_decay, vc)
        return S, o

    S, os_ = lax.scan(step, S, (_chunks(q, T), _chunks(k, T), _chunks(v, T)))
    return _unchunk(os_), S


def _ret_prep(p, rope):
    nqk = RT_HEADS * RT_DK
    q, k, v, g = jnp.split(p, [nqk, 2 * nqk, 2 * nqk + RT_W], axis=-1)
    q = _heads(q, RT_HEADS, RT_DK)
    k = _heads(k, RT_HEADS, RT_DK)
    v = _heads(v, RT_HEADS, RT_DV)
    if rope is not None:
        q = apply_rope(q, rope[0], rope[1])
        k = apply_rope(k, rope[0], rope[1])
    return q * (RT_DK ** -0.5), k, v, g


def _ret_bidir(q, k, v, lg_f, lg_b, init_f, init_b):
    o_f, S_f = retention_scan(q, k, v, lg_f, init_f)
    fl = lambda a: jnp.flip(a, axis=2)
    o_b, S_b = retention_scan(fl(q), fl(k), fl(v), lg_b, init_b)
    return o_f + fl(o_b), S_f, S_b


def _ret_out(o, g):
    B, H, L, dv = o.shape
    on = _head_rms(o).transpose(0, 2, 1, 3).reshape(B, L, H * dv)
    return (on * jax.nn.silu(g.astype(jnp.float32))).astype(g.dtype)


def retention_group(pc, px, log_decay, rope):
    lg_f = -jnp.exp(log_decay[0].astype(jnp.float32))
    lg_b = -jnp.exp(log_decay[1].astype(jnp.float32))
    B = pc.shape[0]
    zero = jnp.zeros((B, RT_HEADS, RT_DK, RT_DV), jnp.float32)
    qc, kc, vc, gc = _ret_prep(pc, None)
    oc, S_f, S_b = _ret_bidir(qc, kc, vc, lg_f, lg_b, zero, zero)
    qx, kx, vx, gx = _ret_prep(px, rope)
    ox, _, _ = _ret_bidir(qx, kx, vx, lg_f, lg_b, S_f, S_b)
    return _ret_out(oc, gc), _ret_out(ox, gx)


def sq_relu_mlp(h, w1, w2):
    return jnp.square(jax.nn.relu(h @ w1)) @ w2


def setup_inputs(seed: int = 0) -> dict:
    key = jax.random.key(seed)
    ks = jax.random.split(key, 32)
    f32 = jnp.float32
    nrm = lambda k, shape, s: jax.random.normal(k, shape, f32) * s
    D = D_MODEL
    i_b = nrm(ks[17], (DEPTH, 2, ML_HEADS), 0.1)
    f_b = jnp.linspace(3.0, 6.0, ML_HEADS, dtype=f32) + nrm(ks[18], (DEPTH, 2, ML_HEADS), 0.1)
    ml_gate_b = jnp.stack([i_b[:, 0], f_b[:, 0], i_b[:, 1], f_b[:, 1]], axis=1).reshape(DEPTH, MLG_COLS)
    base = jnp.log(-jnp.log(1.0 - 2.0 ** (-5.0 - jnp.arange(RT_HEADS, dtype=f32))))
    return {
        "x": nrm(ks[0], (BATCH, SEQ, D), 1.0),
        "c": nrm(ks[1], (BATCH, D), 1.0),
        "ctx": nrm(ks[2], (BATCH, CTX_LEN, D), 1.0),
        "c_ctx": nrm(ks[3], (D,), 1.0),
        "norm1_g": 1.0 + nrm(ks[4], (DEPTH, D), 0.02),
        "norm2_g": 1.0 + nrm(ks[5], (DEPTH, D), 0.02),
        "w_mod": nrm(ks[6], (DEPTH, D, N_MOD * D), D ** -0.5),
        "b_mod": nrm(ks[7], (DEPTH, N_MOD * D), 0.02),
        "w_in": nrm(ks[8], (DEPTH, D, N_IN), D ** -0.5),
        "hy_conv_w": nrm(ks[9], (DEPTH, 3, HY_COLS), 3 ** -0.5),
        "hy_f_w1": nrm(ks[10], (DEPTH, HY_EMB, HY_FF), HY_EMB ** -0.5),
        "hy_f_b1": nrm(ks[11], (DEPTH, HY_FF), 0.02),
        "hy_f_w2": nrm(ks[12], (DEPTH, HY_FF, HY_FF), HY_FF ** -0.5),
        "hy_f_b2": nrm(ks[13], (DEPTH, HY_FF), 0.02),
        "hy_f_w3": nrm(ks[14], (DEPTH, HY_FF, 2 * HY_W), HY_FF ** -0.5),
        "hy_f_freq": 1.0 + nrm(ks[15], (DEPTH, HY_FF), 0.02),
        "hy_bias": nrm(ks[16], (DEPTH, HY_W), 0.5),
        "ml_conv_w": nrm(ks[19], (DEPTH, 3, 2 * ML_W), 3 ** -0.5),
        "ml_gate_b": ml_gate_b,
        "ml_norm_g": 1.0 + nrm(ks[20], (DEPTH, ML_W), 0.02),
        "rt_log_decay": base + nrm(ks[21], (DEPTH, 2, RT_HEADS), 0.05),
        "w_out": nrm(ks[22], (DEPTH, D, D), D ** -0.5),
        "w_ff1": nrm(ks[23], (DEPTH, D, D_FF), D ** -0.5),
        "w_ff2": nrm(ks[24], (DEPTH, D_FF, D), D_FF ** -0.5),
        "final_g": 1.0 + nrm(ks[25], (D,), 0.02),
    }


def reference(x, c, ctx, c_ctx, norm1_g, norm2_g, w_mod, b_mod, w_in, hy_conv_w, hy_f_w1, hy_f_b1,
              hy_f_w2, hy_f_b2, hy_f_w3, hy_f_freq, hy_bias, ml_conv_w, ml_gate_b, ml_norm_g,
              rt_log_decay, w_out, w_ff1, w_ff2, final_g):
    L = x.shape[1]
    Lc = ctx.shape[1]
    rope = axial_rope(L)
    s_lat = jax.nn.silu(c)
    s_ctx = jax.nn.silu(c_ctx)[None]
    h_ctx = ctx
    for l in range(DEPTH):
        need_ctx = l < DEPTH - 1
        mx = jnp.split((s_lat @ w_mod[l] + b_mod[l])[:, None, :], N_MOD, axis=-1)
        mc = jnp.split((s_ctx @ w_mod[l] + b_mod[l])[:, None, :], N_MOD, axis=-1)
        px = modulate(rms_norm(x, norm1_g[l]), mx[0], mx[1]) @ w_in[l]
        pc = modulate(rms_norm(h_ctx, norm1_g[l]), mc[0], mc[1]) @ w_in[l]
        hy_x, ml_x, mg_x, rt_x = jnp.split(px, SPLITS, axis=-1)
        hy_c, ml_c, mg_c, rt_c = jnp.split(pc, SPLITS, axis=-1)
        filt_x = hyena_filter(L, hy_f_w1[l], hy_f_b1[l], hy_f_w2[l], hy_f_b2[l], hy_f_w3[l], hy_f_freq[l])
        y_hy = hyena_mixer(hy_x, hy_conv_w[l], filt_x, hy_bias[l])
        yc_ml, y_ml = mlstm_group(ml_c, mg_c, ml_x, mg_x, ml_conv_w[l], ml_gate_b[l], ml_norm_g[l])
        yc_rt, y_rt = retention_group(rt_c, rt_x, rt_log_decay[l], rope)
        x = x + mx[2] * (jnp.concatenate([y_hy, y_ml, y_rt], axis=-1) @ w_out[l])
        if need_ctx:
            filt_c = hyena_filter(Lc, hy_f_w1[l], hy_f_b1[l], hy_f_w2[l], hy_f_b2[l], hy_f_w3[l], hy_f_freq[l])
            yc_hy = hyena_mixer(hy_c, hy_conv_w[l], filt_c, hy_bias[l])
            h_ctx = h_ctx + mc[2] * (jnp.concatenate([yc_hy, yc_ml, yc_rt], axis=-1) @ w_out[l])
            h_ctx = h_ctx + mc[5] * sq_relu_mlp(modulate(rms_norm(h_ctx, norm2_g[l]), mc[3], mc[4]),
                                                w_ff1[l], w_ff2[l])
        x = x + mx[5] * sq_relu_mlp(modulate(rms_norm(x, norm2_g[l]), mx[3], mx[4]), w_ff1[l], w_ff2[l])
    return rms_norm(x, final_g)
```

```python
import contextlib
import math
import numpy as np
import concourse.bass as bass
import concourse.mybir as mybir
from concourse.bass_utils import run_bass_kernel_spmd

F32 = mybir.dt.float32
BF16 = mybir.dt.bfloat16
I32 = mybir.dt.int32
ALU = mybir.AluOpType
AF = mybir.ActivationFunctionType
AX = mybir.AxisListType

D = 2048
L = 2048
LC = 256
NT = 18
NTOK = NT * 128
DEPTH = 2
HYW = 512
MLW = 768
RTW = 768
DH = 192
NEG = -30000.0
EPS = 1e-6
NFM = 36
TMW = 3344
W2C = NFM * 128 + TMW
TM_GROUPS = [(0, 512), (512, 512), (1024, 512), (1536, 512), (2048, 512), (2560, 512), (3072, 256), (3328, 16)]
GROUPS = [(0, 2), (2, 4), (6, 4), (10, 4), (14, 4)]


class Res:
    __slots__ = ("name", "w", "r", "dsem", "dcnt", "t")

    def __init__(self, name, t=None):
        self.name = name
        self.w = None
        self.r = {}
        self.dsem = None
        self.dcnt = 0
        self.t = t

    def __getitem__(self, idx):
        return self.t[idx]


class Prog:
    ENG = ("pe", "dve", "act", "pool", "sp")

    def __init__(self, nc):
        self.nc = nc
        self.stack = contextlib.ExitStack()
        self.h = {"pe": nc.tensor, "dve": nc.vector, "act": nc.scalar, "pool": nc.gpsimd, "sp": nc.sync}
        self.sems = {}
        for k in self.ENG:
            self.sems[k] = self.stack.enter_context(nc.semaphore("sem_" + k))
        self.cnt = {k: 0 for k in self.ENG}
        self.seen = {k: {} for k in self.ENG}
        self.ninst = {k: 0 for k in self.ENG}
        self.ndsem = 0
        self.out_events = {}
        self.dma_tot = {}
        self.dsem_pool = []

    def sb(self, name, shape, dtype, stk=None):
        self.ndsem_names = getattr(self, "ndsem_names", 0) + 1
        name = "sb%d_%s" % (self.ndsem_names, name)
        t = (stk.es if stk is not None else self.stack).enter_context(self.nc.sbuf_tensor(name, list(shape), dtype))
        r = Res(name, t)
        if stk is not None:
            stk.res.append(r)
        return r

    def scope(self):
        return Scope(self)

    def ps(self, name, shape, dtype=F32):
        t = self.stack.enter_context(self.nc.psum_tensor(name, list(shape), dtype))
        return Res(name, t)

    def res(self, name):
        return Res(name)

    def _deps(self, reads, writes):
        deps = {}
        for r in reads:
            if r.w is not None:
                k, v = r.w
                if deps.get(k, 0) < v:
                    deps[k] = v
        for w in writes:
            if w.w is not None:
                k, v = w.w
                if deps.get(k, 0) < v:
                    deps[k] = v
            for k, v in w.r.items():
                if deps.get(k, 0) < v:
                    deps[k] = v
        return deps

    def _waits(self, e, deps, is_mm=False):
        waits = []
        seen = self.seen[e]
        for k, v in deps.items():
            if is_mm and k == "pe":
                continue
            if seen.get(k, 0) >= v:
                continue
            seen[k] = v
            waits.append((k, v))
        return waits

    def _mark(self, ev, reads, writes):
        k, v = ev
        for r in reads:
            if r.r.get(k, 0) < v:
                r.r[k] = v
        for w in writes:
            w.w = ev
            w.r = {}

    def _emit(self, e, fn, waits, inc):
        h = self.h[e]
        for k, v in waits:
            h.wait_ge(self.sems[k], v)
            self.ninst[e] += 1
        if fn is None:
            return
        ins = fn(h)
        self.ninst[e] += 1
        if inc is not None:
            ins.then_inc(self.sems[inc[0]], inc[1])

    def op(self, e, fn, reads=(), writes=(), is_mm=False):
        deps = self._deps(reads, writes)
        waits = self._waits(e, deps, is_mm)
        self.cnt[e] += 1
        ev = (e, self.cnt[e])
        self._emit(e, fn, waits, (e, 1))
        self._mark(ev, reads, writes)
        return ev

    def dma(self, q, out_ap, in_ap, reads, writes, semres, out_final=False, **kw):
        deps = self._deps(reads, writes)
        waits = self._waits(q, deps)
        if semres.dsem is None:
            if self.dsem_pool:
                semres.dsem, semres.dcnt = self.dsem_pool.pop()
            else:
                semres.dsem = self.stack.enter_context(self.nc.semaphore("dsem%d" % self.ndsem))
                self.ndsem += 1
        semres.dcnt += 16
        key = ("d", semres.dsem.num)
        self.sems[key] = semres.dsem
        ev = (key, semres.dcnt)
        self.dma_tot[key] = semres.dcnt
        self._emit(q, lambda h: h.dma_start(out=out_ap, in_=in_ap, **kw), waits, (key, 16))
        self._mark(ev, reads, writes)
        if out_final:
            self.out_events[key] = semres.dcnt
        return ev

    def barrier(self):
        allev = {k: self.cnt[k] for k in self.ENG if self.cnt[k] > 0}
        allev.update(self.dma_tot)
        for e in self.ENG:
            waits = self._waits(e, dict(allev))
            self._emit(e, None, waits, None)

    def finish(self):
        allev = {k: self.cnt[k] for k in self.ENG if self.cnt[k] > 0}
        allev.update(self.dma_tot)
        waits = self._waits("sp", allev)
        self._emit("sp", None, waits, None)

    def mm(self, out_ap, lhsT, rhs, start, stop, reads, writes):
        return self.op("pe", lambda h: h.matmul(out_ap, lhsT, rhs, start=start, stop=stop), reads, writes, is_mm=True)

    def tr(self, out_ap, in_ap, ident_ap, reads, writes):
        return self.op("pe", lambda h: h.transpose(out_ap, in_ap, ident_ap), reads, writes, is_mm=True)

    def act(self, out_ap, in_ap, func, reads, writes, bias=None, scale=None, accum_out=None):
        kw = {}
        if bias is not None:
            kw["bias"] = bias
        if scale is not None:
            kw["scale"] = scale
        if accum_out is not None:
            kw["accum_out"] = accum_out
        return self.op("act", lambda h: h.activation(out_ap, in_ap, func, **kw), reads, writes)

    def tt(self, e, out_ap, in0, in1, op, reads, writes):
        return self.op(e, lambda h: h.tensor_tensor(out_ap, in0, in1, op), reads, writes)

    def ts(self, e, out_ap, in0, s1, s2, op0, op1=None, reads=(), writes=()):
        kw = {}
        if op1 is not None:
            kw["op1"] = op1
        return self.op(e, lambda h: h.tensor_scalar(out_ap, in0, s1, s2, op0, **kw), reads, writes)

    def stt(self, e, out_ap, in0, scalar, in1, op0, op1, reads, writes):
        return self.op(e, lambda h: h.scalar_tensor_tensor(out_ap, in0, scalar, in1, op0, op1), reads, writes)

    def copy(self, e, out_ap, in_ap, reads, writes):
        if e == "act":
            return self.op(e, lambda h: h.copy(out_ap, in_ap), reads, writes)
        return self.op(e, lambda h: h.tensor_copy(out_ap, in_ap), reads, writes)

    def memset(self, e, ap, val, writes):
        return self.op(e, lambda h: h.memset(ap, val), (), writes)


class Scope:
    def __init__(self, prog):
        self.p = prog
        self.es = contextlib.ExitStack()
        self.res = []

    def __enter__(self):
        self.es.__enter__()
        return self

    def __exit__(self, *a):
        if a[0] is None:
            self.p.barrier()
            for r in self.res:
                if r.dsem is not None:
                    self.p.dsem_pool.append((r.dsem, r.dcnt))
                    r.dsem = None
        return self.es.__exit__(*a)


class Rot:
    def __init__(self, tiles):
        self.tiles = tiles
        self.i = 0

    def next(self):
        t = self.tiles[self.i % len(self.tiles)]
        self.i += 1
        return t


def fm_src_cols():
    src = np.full((NFM, 128), -1, np.int64)
    for j in range(4):
        src[3 * j] = np.arange(128) + j * 128
        src[3 * j + 1] = 512 + np.arange(128) + j * 128
        src[3 * j + 2] = 1024 + np.arange(128) + j * 128
    for part, base in ((0, 1536), (1, 1536 + 768)):
        for h in range(4):
            c0 = 12 + part * 8 + 2 * h
            src[c0] = base + h * 192 + np.arange(128)
            src[c0 + 1, :64] = base + h * 192 + 128 + np.arange(64)
    rt = 1536 + 3072 + 16
    for part, base in ((0, rt), (1, rt + 384)):
        for h in range(4):
            ca = 28 + 4 * part + 2 * (h // 2)
            lo = (h % 2) * 48
            src[ca, lo:lo + 48] = base + h * 96 + np.arange(48)
            src[ca + 1, lo:lo + 48] = base + h * 96 + 48 + np.arange(48)
    return src


def tm_src_cols():
    rt = 1536 + 3072 + 16
    src = np.full((TMW,), -1, np.int64)
    src[0:768] = 1536 + 1536 + np.arange(768)
    src[768:1536] = rt + 768 + np.arange(768)
    src[1536:2304] = 1536 + 2304 + np.arange(768)
    src[2560:3328] = rt + 1536 + np.arange(768)
    src[3328:3344] = 1536 + 3072 + np.arange(16)
    return src


def build_w2(w_in_l):
    src = np.concatenate([fm_src_cols().reshape(-1), tm_src_cols()])
    w2 = np.zeros((D, W2C), np.float32)
    ok = src >= 0
    w2[:, ok] = w_in_l[:, src[ok]]
    return w2


def build_convw(hy_conv_l, ml_conv_l):
    src = fm_src_cols()
    cw = np.zeros((128, 28, 3), np.float32)
    for cc in range(28):
        for p in range(128):
            s = src[cc, p]
            if s < 0:
                continue
            if cc < 12:
                cw[p, cc, :] = hy_conv_l[:, s]
            else:
                cw[p, cc, :] = ml_conv_l[:, s - 1536]
    return cw


def dft_tables(n):
    N = 2 * n
    k = np.arange(n, dtype=np.float64) + 0.5
    ang = 2.0 * np.pi * np.outer(k, k) / N
    cw = np.cos(np.pi * k / N)
    sw = np.sin(np.pi * k / N)
    return np.cos(ang).astype(np.float32), np.sin(ang).astype(np.float32), cw.astype(np.float32), sw.astype(np.float32)


def hyena_consts(n):
    f32 = np.float32
    t = np.linspace(0.0, 1.0, n, dtype=f32)[:, None]
    w = (f32(2.0 * math.pi / n)) * np.arange(n, dtype=f32)[:, None]
    bands = np.linspace(1e-4, 15, 16, dtype=f32)[None, :]
    z = np.concatenate([t, np.cos(bands * w), -np.sin(bands * w)], axis=-1).astype(f32)
    deltas = np.abs(np.linspace(math.log(1e-2) / 1.5, math.log(1e-2) / 0.3, HYW, dtype=f32))
    win = np.exp(-t * deltas).astype(f32)
    return np.ascontiguousarray(z.T), win


def rope_tables():
    rows = L // 64
    r = np.repeat(np.arange(rows, dtype=np.float32), 64)
    col = np.tile(np.arange(64, dtype=np.float32), rows)
    nf = 24
    inv = (10000.0 ** (-np.arange(nf, dtype=np.float32) / nf)).astype(np.float32)
    ang = np.concatenate([r[:, None] * inv, col[:, None] * inv], axis=-1)
    cos = np.cos(ang).astype(np.float32).T
    sin = np.sin(ang).astype(np.float32).T
    z = np.zeros((32, L), np.float32)
    return np.concatenate([cos, cos, z]), np.concatenate([sin, sin, z])


def attn_consts():
    ident = np.eye(128, dtype=np.float32)
    r = np.arange(128)
    tri_f = (r[:, None] <= r[None, :]).astype(np.float32)
    tri_b = (r[:, None] >= r[None, :]).astype(np.float32)
    mask = np.zeros((2, 4, 128, 512), np.float32)
    for j in range(4):
        for i in range(4):
            blk = slice(i * 128, (i + 1) * 128)
            if i < j:
                mask[0, j, :, blk] = NEG
            elif i == j:
                mask[0, j, :, blk] = np.where(r[None, :] >= r[:, None], 0.0, NEG)
            if i > j:
                mask[1, j, :, blk] = NEG
            elif i == j:
                mask[1, j, :, blk] = np.where(r[None, :] <= r[:, None], 0.0, NEG)
    vpos = np.zeros((128, 2, NT), np.float32)
    for c in range(NT):
        vpos[:, 0, c] = c * 128 + r + 1
    order_b = [1, 0] + list(range(17, 1, -1))
    for vi, c in enumerate(order_b):
        vpos[:, 1, c] = vi * 128 + (127 - r) + 1
    hmask = np.zeros((128, 4), np.float32)
    for h in range(4):
        hmask[(h % 2) * 48:(h % 2) * 48 + 48, h] = 1.0
    return ident, tri_f, tri_b, mask, vpos, hmask


def build_program(debug=False, stage=99):
    nc = bass.Bass("TRN2", target_bir_lowering=False)
    p = Prog(nc)

    def din(name, shape, dt=F32):
        return nc.dram_tensor(name, list(shape), dt, kind="ExternalInput").ap()

    def dscr(name, shape, dt):
        kind = "ExternalOutput" if debug else "Internal"
        return nc.dram_tensor(name, list(shape), dt, kind=kind).ap()

    xin = din("xin", [NTOK, D])
    c2 = din("c2", [128, 16, 2])
    norm1_g = din("norm1_g", [DEPTH, D])
    norm2_g = din("norm2_g", [DEPTH, D])
    final_g = din("final_g", [1, D])
    w_mod = din("w_mod", [DEPTH, D, 6 * D])
    b_mod = din("b_mod", [DEPTH, 6 * D])
    w2 = din("w2", [DEPTH, D, W2C])
    convw = din("convw", [DEPTH, 128, 28, 3])
    w_out = din("w_out", [DEPTH, D, D])
    w_ff1 = din("w_ff1", [DEPTH, D, 4 * D])
    w_ff2 = din("w_ff2", [DEPTH, 4 * D, D])
    hy_w1 = din("hy_w1", [DEPTH, 33, 64])
    hy_w2 = din("hy_w2", [DEPTH, 64, 64])
    hy_w3 = din("hy_w3", [DEPTH, 64, 1024])
    hy_vec = din("hy_vec", [DEPTH, 64, 3])
    hy_bias = din("hy_bias", [DEPTH, 128, 4])
    gateb_t = din("gateb_t", [DEPTH, NT * 16])
    ml_norm_g = din("ml_norm_g", [DEPTH, 768])
    rt_ld = din("rt_ld", [DEPTH, 8])
    ident_in = din("ident", [128, 128])
    tri_in = din("tri", [2, 128, 128])
    mask_in = din("mask", [2, 4, 128, 512])
    vpos_in = din("vpos", [128, 2, NT])
    hmask_in = din("hmask", [128, 4])
    rope_in = din("rope", [2, 128, L])
    zx_in = din("zx", [33, L])
    zc_in = din("zc", [33, LC])
    winx_in = din("winx", [L, HYW])
    winc_in = din("winc", [LC, HYW])
    dftx_in = din("dftx", [2, L, L])
    dftc_in = din("dftc", [2, LC, LC])
    hwx_in = din("hwx", [128, 16, 2])
    hwc_in = din("hwc", [128, 2, 2])
    yout = nc.dram_tensor("yout", [L, D], F32, kind="ExternalOutput").ap()
    xs = dscr("xs", [NTOK, D], F32)
    modv = dscr("modv", [DEPTH, 2, 6 * D], F32)
    fmq = dscr("fmq", [NFM, 128, NTOK], BF16)
    hyz = dscr("hyz", [4, 128, NTOK], BF16)
    tmv = dscr("tmv", [NTOK, 1536], BF16)
    tmo = dscr("tmo", [NTOK, 1024], BF16)
    tmg = dscr("tmg", [NTOK, 768], BF16)
    tgate = dscr("tgate", [NTOK, 16], F32)
    R = {k: p.res(k) for k in ("xs", "modv", "fmq", "hyz", "tmv", "tmo", "tmg", "tgate", "yout")}
    xs_t = [p.res("xs%d" % i) for i in range(NT)]

    ident_f = p.sb("ident_f", [128, 128], F32)
    ident_b = p.sb("ident_b", [128, 128], BF16)
    ones_f = p.sb("ones_f", [128, 128], F32)
    ones_b = p.sb("ones_b", [128, 128], BF16)
    bufA = p.sb("bufA", [128, 16 * NTOK], BF16)
    hT = bufA.t[:, :].rearrange("p (k t) -> p k t", k=16)
    PS = [p.ps("psb%d" % i, [128, 512]) for i in range(8)]

    p.dma("sp", ident_f[:], ident_in, [], [ident_f], ident_f)
    p.copy("dve", ident_b[:], ident_f[:], [ident_f], [ident_b])
    p.memset("dve", ones_f[:], 1.0, [ones_f])
    p.memset("dve", ones_b[:], 1.0, [ones_b])

    wsem_i = [0]

    def rstd_from_sumsq(dst, src, n, reads_writes):
        p.act(dst, src, AF.Ln, reads_writes, reads_writes, scale=1.0 / n, bias=eps_t[:, 0:1])
        p.act(dst, dst, AF.Exp, reads_writes, reads_writes, scale=-0.5)

    eps_t = p.sb("eps_t", [128, 1], F32)
    p.memset("dve", eps_t[:], EPS, [eps_t])
    one_t = p.sb("one_t", [128, 1], F32)
    p.memset("dve", one_t[:], 1.0, [one_t])

    with p.scope() as stk:
        c2t = p.sb("c2t", [128, 16, 2], F32, stk)
        sT = p.sb("sT", [128, 16, 2], BF16, stk)
        p.dma("sp", c2t[:], c2, [], [c2t], c2t)
        p.act(sT[:], c2t[:], AF.Silu, [c2t], [sT])
        wrot = Rot([p.sb("wm%d" % i, [128, 16, 512], BF16, stk) for i in range(3)])
        brot = Rot([p.sb("bm%d" % i, [2, 512], F32, stk) for i in range(2)])
        orot = Rot([p.sb("om%d" % i, [2, 512], F32, stk) for i in range(2)])
        prot = Rot([PS[0], PS[1]])
        for l in range(DEPTH):
            wv = w_mod[l].rearrange("(kc p) n -> p kc n", p=128)
            for g in range(24):
                wt = wrot.next()
                p.dma("pool", wt[:], wv[:, :, g * 512:(g + 1) * 512], [], [wt], wt)
                bt = brot.next()
                p.dma("sp", bt[:], b_mod[l:l + 1, g * 512:(g + 1) * 512].partition_broadcast(2), [], [bt], bt)
                ps = prot.next()
                for kc in range(16):
                    p.mm(ps[0:2, :], sT[:, kc, :], wt[:, kc, :], kc == 0, kc == 15, [sT, wt], [ps])
                ot = orot.next()
                p.tt("dve", ot[:], ps[0:2, :], bt[:], ALU.add, [ps, bt], [ot])
                p.dma("sp", modv[l, :, g * 512:(g + 1) * 512], ot[:], [ot], [R["modv"]], ot)
    p.barrier()
    if stage == 0:
        p.finish()
        return nc, p

    hT_t = [p.res("hT%d" % i) for i in range(NT)]
    ps_bf = PS[7].t[:, :].bitcast(BF16)

    def bcast_row(dst, row_ap, q="sp"):
        p.dma(q, dst[:], row_ap.partition_broadcast(128), [R["modv"]], [dst], dst)

    def make_mod_tiles(stk, l, r, j_shift, j_scale, g_row, tmp_g, tmp_s, tag):
        A = p.sb("A_%s" % tag, [128, D], F32, stk)
        sh = p.sb("sh_%s" % tag, [128, D], F32, stk)
        bcast_row(tmp_g, g_row)
        bcast_row(tmp_s, modv[l, r:r + 1, j_scale * D:(j_scale + 1) * D])
        p.stt("dve", A[:], tmp_s[:], 1.0, tmp_g[:], ALU.add, ALU.mult, [tmp_s, tmp_g], [A])
        bcast_row(sh, modv[l, r:r + 1, j_shift * D:(j_shift + 1) * D])
        return A, sh

    def norm_mod_tile(xt, A, sh, hb, ss, junk):
        p.memset("dve", ss[:], 0.0, [ss])
        p.act(junk[:], xt[:], AF.Square, [xt], [junk, ss], accum_out=ss[:, 0:1])
        rstd_from_sumsq(ss[:, 0:1], ss[:, 0:1], D, [ss])
        p.stt("dve", xt[:], xt[:], ss[:, 0:1], A[:], ALU.mult, ALU.mult, [xt, ss, A], [xt])
        p.tt("dve", hb[:], xt[:], sh[:], ALU.add, [xt, sh], [hb])

    def transpose_to(hb, dst_view, dst_res, tcol0):
        for q4 in range(4):
            for j in range(4):
                k = q4 * 4 + j
                p.tr(ps_bf[:, j * 128:(j + 1) * 128], hb[:, k * 128:(k + 1) * 128], ident_b[:], [hb, ident_b], [PS[7]])
            src = ps_bf[:, 0:512].rearrange("p (k t) -> p k t", k=4)
            p.copy("act", dst_view[:, q4 * 4:(q4 + 1) * 4, tcol0:tcol0 + 128], src, [PS[7]], [dst_res])

    def x_src(l, i):
        if l == 0:
            return xin[i * 128:(i + 1) * 128, :], []
        return xs[i * 128:(i + 1) * 128, :], [xs_t[i]]

    def layer_front(l):
        with p.scope() as stk:
            tmp_g = p.sb("tmp_g", [128, D], F32, stk)
            tmp_s = p.sb("tmp_s", [128, D], F32, stk)
            A_c, sh_c = make_mod_tiles(stk, l, 1, 0, 1, norm1_g[l:l + 1, :], tmp_g, tmp_s, "c")
            A_x, sh_x = make_mod_tiles(stk, l, 0, 0, 1, norm1_g[l:l + 1, :], tmp_g, tmp_s, "x")
            xrot = Rot([p.sb("xt%d" % i, [128, D], F32, stk) for i in range(2)])
            hrot = Rot([p.sb("hb%d" % i, [128, D], BF16, stk) for i in range(2)])
            junk = p.sb("junk", [128, D], BF16, stk)
            ssr = Rot([p.sb("ss%d" % i, [128, 1], F32, stk) for i in range(2)])
            for i in range(NT):
                xt = xrot.next()
                src, rd = x_src(l, i)
                p.dma("sp", xt[:], src, rd, [xt], xt)
                hb = hrot.next()
                A, sh = (A_c, sh_c) if i < 2 else (A_x, sh_x)
                sst = ssr.next()
                norm_mod_tile(xt, A, sh, hb, sst, junk)
                if debug and i == 2 and l == 0:
                    p.dma("sp", dbg_ss, sst[:], [sst], [R["yout"]], sst)
                    p.dma("sp", dbg_h, hb[:], [hb], [R["yout"]], hb)
                    p.dma("sp", dbg_x, xt[:], [xt], [R["yout"]], xt)
                transpose_to(hb, hT, hT_t[i], i * 128)
            if stage == -1:
                p.finish()
                return "stop"
        p.barrier()
        with p.scope() as stk:
            w2v = w2[l].rearrange("(kc p) n -> p kc n", p=128)
            wrot = Rot([p.sb("wi%d" % i, [128, 16, 512], BF16, stk) for i in range(2)])
            rbrot = Rot([p.sb("rb%d" % i, [128, 2308], F32, stk) for i in range(2)])
            u = p.sb("u", [128, 2308], F32, stk)
            ux1 = p.sb("ux1", [128, 2308], F32, stk)
            strot = Rot([p.sb("stg%d" % i, [128, NTOK], BF16, stk) for i in range(2)])
            tmrot = Rot([p.sb("tms%d" % i, [128, 512], BF16, stk) for i in range(3)])
            tgrot = Rot([p.sb("tgs%d" % i, [128, 16], F32, stk) for i in range(2)])
            cw = p.sb("cw", [128, 28, 3], F32, stk)
            ropec = p.sb("ropec", [128, L], F32, stk)
            ropes = p.sb("ropes", [128, L], F32, stk)
            p.dma("sp", cw[:], convw[l], [], [cw], cw)
            p.dma("sp", ropec[:], rope_in[0], [], [ropec], ropec)
            p.dma("sp", ropes[:], rope_in[1], [], [ropes], ropes)
            for rb in rbrot.tiles:
                p.memset("dve", rb[:], 0.0, [rb])
            p.memset("dve", u[:], 0.0, [u])
            psrot = Rot([PS[0], PS[1], PS[2], PS[3]])

            def conv3(rb, cc, dst):
                p.ts("dve", dst[:, 1:2307], rb[:, 1:2307], cw[:, cc, 1:2], None, ALU.mult, reads=[rb, cw], writes=[dst])
                p.stt("dve", dst[:, 1:2307], rb[:, 0:2306], cw[:, cc, 0:1], dst[:, 1:2307], ALU.mult, ALU.add, [rb, cw, dst], [dst])
                p.stt("dve", dst[:, 1:2307], rb[:, 2:2308], cw[:, cc, 2:3], dst[:, 1:2307], ALU.mult, ALU.add, [rb, cw, dst], [dst])

            def to_stage(e, st, srcbuf, with_ctx, func=None):
                parts = ([(0, 256, 1)] if with_ctx else []) + [(256, 2048, 259)]
                for (d0, n, s0) in parts:
                    if func is None:
                        p.copy(e, st[:, d0:d0 + n], srcbuf[:, s0:s0 + n], [srcbuf], [st])
                    else:
                        p.act(st[:, d0:d0 + n], srcbuf[:, s0:s0 + n], func, [srcbuf], [st])

            def store_fm(dst_ap, dres, st, with_ctx):
                if with_ctx:
                    p.dma("sp", dst_ap, st[:], [st], [dres], st)
                else:
                    p.dma("sp", dst_ap[:, 256:NTOK], st[:, 256:NTOK], [st], [dres], st)

            wt = None
            rbA = None
            for fc in range(NFM):
                if fc % 4 == 0:
                    wt = wrot.next()
                    p.dma("pool", wt[:], w2v[:, :, fc * 128:fc * 128 + 512], [], [wt], wt)
                with_ctx = not (l == 1 and fc < 12)
                rb = rbrot.next()
                for (t0, nt) in GROUPS:
                    if t0 == 0 and not with_ctx:
                        continue
                    N = nt * 128
                    ps = psrot.next()
                    for kc in range(16):
                        p.mm(ps[:, 0:N], wt[:, kc, (fc % 4) * 128:(fc % 4) * 128 + 128], hT[:, kc, t0 * 128:t0 * 128 + N],
                             kc == 0, kc == 15, [wt] + hT_t[t0:t0 + nt], [ps])
                    o0 = 1 if t0 == 0 else 3 + t0 * 128
                    p.copy("act", rb[:, o0:o0 + N], ps[:, 0:N], [ps], [rb])
                if fc < 12:
                    j, kind = fc // 3, fc % 3
                    if kind == 0:
                        conv3(rb, fc, u)
                        st = strot.next()
                        to_stage("dve", st, u, with_ctx)
                        store_fm(fmq[fc], R["fmq"], st, with_ctx)
                    elif kind == 1:
                        conv3(rb, fc, ux1)
                    else:
                        conv3(rb, fc, u)
                        p.tt("dve", u[:, 1:2307], u[:, 1:2307], ux1[:, 1:2307], ALU.mult, [u, ux1], [u])
                        st = strot.next()
                        to_stage("dve", st, u, with_ctx)
                        store_fm(hyz[j], R["hyz"], st, with_ctx)
                elif fc < 28:
                    conv3(rb, fc, u)
                    st = strot.next()
                    to_stage("act", st, u, True, AF.Silu)
                    store_fm(fmq[fc], R["fmq"], st, True)
                else:
                    if fc % 2 == 0:
                        rbA = rb
                    else:
                        rbB = rb
                        la = slice(259, 2307)
                        p.tt("dve", u[:, la], rbA[:, la], ropec[:], ALU.mult, [rbA, ropec], [u])
                        p.tt("dve", ux1[:, la], rbB[:, la], ropes[:], ALU.mult, [rbB, ropes], [ux1])
                        p.tt("dve", u[:, la], u[:, la], ux1[:, la], ALU.subtract, [u, ux1], [u])
                        p.copy("dve", u[:, 1:257], rbA[:, 1:257], [rbA], [u])
                        st = strot.next()
                        to_stage("dve", st, u, True)
                        store_fm(fmq[fc - 1], R["fmq"], st, True)
                        p.tt("dve", u[:, la], rbA[:, la], ropes[:], ALU.mult, [rbA, ropes], [u])
                        p.tt("dve", ux1[:, la], rbB[:, la], ropec[:], ALU.mult, [rbB, ropec], [ux1])
                        p.tt("dve", u[:, la], u[:, la], ux1[:, la], ALU.add, [u, ux1], [u])
                        p.copy("dve", u[:, 1:257], rbB[:, 1:257], [rbB], [u])
                        st = strot.next()
                        to_stage("dve", st, u, True)
                        store_fm(fmq[fc], R["fmq"], st, True)
            for (off, wdt) in TM_GROUPS:
                wt = wrot.next()
                p.dma("pool", wt[:, :, 0:wdt], w2v[:, :, NFM * 128 + off:NFM * 128 + off + wdt], [], [wt], wt)
                for i in range(NT):
                    ps = psrot.next()
                    for kc in range(16):
                        p.mm(ps[:, 0:wdt], hT[:, kc, i * 128:(i + 1) * 128], wt[:, kc, 0:wdt], kc == 0, kc == 15,
                             [wt, hT_t[i]], [ps])
                    rows = slice(i * 128, (i + 1) * 128)
                    if off >= 3328:
                        tg = tgrot.next()
                        p.copy("act", tg[:], ps[:, 0:16], [ps], [tg])
                        p.dma("sp", tgate[rows, :], tg[:], [tg], [R["tgate"]], tg)
                        continue
                    st = tmrot.next()
                    if off < 1536:
                        p.copy("act", st[:, 0:wdt], ps[:, 0:wdt], [ps], [st])
                        p.dma("sp", tmv[rows, off:off + wdt], st[:, 0:wdt], [st], [R["tmv"]], st)
                    elif off < 2560:
                        p.act(st[:, 0:wdt], ps[:, 0:wdt], AF.Sigmoid, [ps], [st])
                        p.dma("sp", tmo[rows, off - 1536:off - 1536 + wdt], st[:, 0:wdt], [st], [R["tmo"]], st)
                    else:
                        p.act(st[:, 0:wdt], ps[:, 0:wdt], AF.Silu, [ps], [st])
                        p.dma("sp", tmg[rows, off - 2560:off - 2560 + wdt], st[:, 0:wdt], [st], [R["tmg"]], st)
        p.barrier()

    ytm = bufA.t[:, 0:NT * 1536].rearrange("p (c n) -> p c n", c=NT)
    yhT = bufA.t[:, NT * 1536:16 * NTOK].rearrange("p (j t) -> p j t", j=4)
    ytm_t = [p.res("ytm%d" % i) for i in range(NT)]
    yh_res = p.res("yhT")

    def mixers(l):
        with p.scope() as mstk:
            masks = [[p.sb("mask%d%d" % (d, j), [128, 512], F32, mstk) for j in range(4)] for d in range(2)]
            tris = [p.sb("tri%d" % d, [128, 128], F32, mstk) for d in range(2)]
            vpos = p.sb("vpos", [128, 2, NT], F32, mstk)
            hmask = p.sb("hmask", [128, 4], F32, mstk)
            for d in range(2):
                p.dma("sp", tris[d][:], tri_in[d], [], [tris[d]], tris[d])
                for j in range(4):
                    p.dma("sp", masks[d][j][:], mask_in[d, j], [], [masks[d][j]], masks[d][j])
            p.dma("sp", vpos[:], vpos_in, [], [vpos], vpos)
            p.dma("sp", hmask[:], hmask_in, [], [hmask], hmask)
            Bt = p.sb("Bt", [128, 2, NT, 4], F32, mstk)
            at = p.sb("at", [128, 2, NT, 4], F32, mstk)
            acc_sets = [[(PS[0], 0), (PS[1], 0), (PS[2], 0), (PS[3], 0)]] * 2
            acc_res = [[PS[0], PS[1], PS[2], PS[3]]] * 2
            acc_i = [0]
            psS = Rot([PS[4], PS[5], PS[7]])
            psB = PS[6]
            bbrot = Rot([p.sb("bbc%d" % i, [128, 512], F32, mstk) for i in range(2)])
            dgrot = Rot([p.sb("dg%d" % i, [128, 512], F32, mstk) for i in range(1)])
            tmrot2 = Rot([p.sb("tmpm%d" % i, [128, 512], F32, mstk) for i in range(4)])
            dtrot = Rot([p.sb("dt%d" % i, [128, 512], BF16, mstk) for i in range(4)])
            smrot = Rot([p.sb("sm%d" % i, [128, 512], BF16, mstk) for i in range(4)])
            hsum = p.sb("hsum", [128, 4, DH], F32, mstk)
            hrrot = Rot([p.sb("hraw%d" % i, [128, 4, DH + 1], F32, mstk) for i in range(2)])
            junk2 = p.sb("junk2", [128, DH], BF16, mstk)
            strot = Rot([p.sb("st8_%d" % i, [128, 8], F32, mstk) for i in range(4)])
            qkrot = Rot([p.sb("qk%d" % i, [128, NTOK], BF16, mstk) for i in range(8)])

            def attn_core(kind, h, qlo, qhi, klo, khi, vaug, ncol, scale, finalize):
                for (t0, w) in GROUPS:
                    if t0 == 0 and l == 1:
                        continue
                    wn = w * 128
                    for d in range(2):
                        dg = dgrot.next()
                        for j in range(w):
                            p.ts("dve", dg[:, j * 128:(j + 1) * 128], ident_f[:], Bt[:, d, t0 + j, h:h + 1], None, ALU.mult,
                                 reads=[ident_f, Bt], writes=[dg])
                        p.mm(psB[:, 0:wn], ones_f[:], dg[:, 0:wn], True, True, [ones_f, dg], [psB])
                        bb = bbrot.next()
                        p.copy("act", bb[:, 0:wn], psB[:, 0:wn], [psB], [bb])
                        if d == 0:
                            steps = [(c, None) for c in range(t0)]
                        elif t0 == 0:
                            steps = []
                        else:
                            steps = [(0, None), (1, None)] + [(c, None) for c in range(t0 + w, NT)]
                        steps += [(t0 + j, j) for j in range(w)]
                        nst = len(steps)
                        tms = []
                        for j in range(w):
                            tm = tmrot2.next()
                            p.tt("dve", tm[:, 0:wn], bb[:, 0:wn], masks[d][j][:, 0:wn], ALU.add, [bb, masks[d][j]], [tm])
                            tms.append(tm)
                        aset = acc_i[0] % 2
                        acc_i[0] += 1
                        accv = [acc_sets[aset][j][0].t[:, acc_sets[aset][j][1]:acc_sets[aset][j][1] + 256] for j in range(4)]
                        accr = acc_res[aset]

                        def stage1(si):
                            c, mj = steps[si]
                            src = bb if mj is None else tms[mj]
                            dt = dtrot.next()
                            p.act(dt[:, 0:wn], src[:, 0:wn], AF.Exp, [src, at], [dt], bias=at[:, d, c, h:h + 1])
                            ps = psS.next()
                            p.mm(ps[:, 0:wn], klo[:, c * 128:(c + 1) * 128], qlo[:, t0 * 128:t0 * 128 + wn], True, False, [klo, qlo], [ps])
                            p.mm(ps[:, 0:wn], khi[:, c * 128:(c + 1) * 128], qhi[:, t0 * 128:t0 * 128 + wn], False, True, [khi, qhi], [ps])
                            sm = smrot.next()
                            p.stt("dve", sm[:, 0:wn], ps[:, 0:wn], scale, dt[:, 0:wn], ALU.mult, ALU.mult, [ps, dt], [sm])
                            return sm

                        def stage2(si, sm):
                            c = steps[si][0]
                            for j in range(w):
                                p.mm(accv[j][:, 0:ncol], sm[:, j * 128:(j + 1) * 128], vaug[:, c, 0:ncol], si == 0, si == nst - 1,
                                     [sm, vaug], [accr[j]])
                        pend = []
                        for si in range(nst):
                            pend.append((si, stage1(si)))
                            if len(pend) > 2:
                                stage2(*pend.pop(0))
                        while pend:
                            stage2(*pend.pop(0))
                        hr = hrrot.next()
                        for j in range(w):
                            p.copy("act", hr[:, j, 0:ncol], accv[j][:, 0:ncol], [accr[j]], [hr])
                        if kind == "ml":
                            s8 = strot.next()
                            p.stt("dve", s8[:, 4:4 + w], hr[:, 0:w, DH], -1.0, hr[:, 0:w, DH], ALU.mult, ALU.max, [hr], [s8])
                            p.ts("dve", s8[:, 4:4 + w], s8[:, 4:4 + w], 1.0, None, ALU.max, reads=[s8], writes=[s8])
                            p.op("dve", lambda hh, o=s8[:, 0:w], i_=s8[:, 4:4 + w]: hh.reciprocal(o, i_), [s8], [s8])
                            for j in range(w):
                                if d == 0:
                                    p.ts("dve", hsum[:, j, :], hr[:, j, 0:DH], s8[:, j:j + 1], None, ALU.mult, reads=[hr, s8], writes=[hsum])
                                else:
                                    p.stt("dve", hsum[:, j, :], hr[:, j, 0:DH], s8[:, j:j + 1], hsum[:, j, :], ALU.mult, ALU.add,
                                          [hr, s8, hsum], [hsum])
                        else:
                            if d == 0:
                                p.copy("dve", hsum[:, 0:w, :], hr[:, 0:w, 0:DH], [hr], [hsum])
                            else:
                                p.tt("dve", hsum[:, 0:w, :], hr[:, 0:w, 0:DH], hsum[:, 0:w, :], ALU.add, [hr, hsum], [hsum])
                    for j in range(w):
                        s8 = strot.next()
                        p.memset("dve", s8[:, 0:1], 0.0, [s8])
                        p.act(junk2[:], hsum[:, j, :], AF.Square, [hsum], [junk2, s8], accum_out=s8[:, 0:1])
                        rstd_from_sumsq(s8[:, 0:1], s8[:, 0:1], DH, [s8])
                        finalize(t0 + j, j, s8)

            with p.scope() as stk:
                gt = p.sb("gt", [128, NT, 16], F32, stk)
                gbb = p.sb("gbb", [128, NT, 16], F32, stk)
                lf = p.sb("lf", [128, 2, NT, 4], F32, stk)
                it = p.sb("it", [128, 2, NT, 4], F32, stk)
                ngb = p.sb("ngb", [128, MLW], F32, stk)
                osig = p.sb("osig", [128, NT, MLW], BF16, stk)
                vrot = Rot([p.sb("vaug%d" % i, [128, NT, DH + 1], BF16, stk) for i in range(2)])
                p.dma("sp", gt[:], tgate.rearrange("(c p) n -> p c n", p=128), [R["tgate"]], [gt], gt)
                p.dma("sp", gbb[:], gateb_t[l:l + 1, :].partition_broadcast(128), [], [gbb], gbb)
                p.dma("sp", ngb[:], ml_norm_g[l:l + 1, :].partition_broadcast(128), [], [ngb], ngb)
                p.dma("sp", osig[:], tmo.rearrange("(c p) n -> p c n", p=128)[:, :, 0:MLW], [R["tmo"]], [osig], osig)
                p.tt("dve", gt[:], gt[:], gbb[:], ALU.add, [gt, gbb], [gt])
                for d in range(2):
                    p.copy("dve", it[:, d, :, :], gt[:, :, d * 8:d * 8 + 4], [gt], [it])
                    p.act(lf[:, d, :, :], gt[:, :, d * 8 + 4:d * 8 + 8], AF.Exp, [gt], [lf], scale=-1.0)
                p.act(lf[:], lf[:], AF.Ln, [lf], [lf], bias=one_t[:, 0:1])
                p.ts("dve", lf[:], lf[:], -1.0, None, ALU.mult, reads=[lf], writes=[lf])
                order = [list(range(NT)), [1, 0] + list(range(17, 1, -1))]
                for d in range(2):
                    for vi, c in enumerate(order[d]):
                        col = (d * NT + c) * 4
                        for pi, cp in enumerate(order[d][:vi]):
                            p.mm(psB[:, col:col + 4], ones_f[:], lf[:, d, cp, :], pi == 0, False, [ones_f, lf], [psB])
                        p.mm(psB[:, col:col + 4], tris[d][:], lf[:, d, c, :], vi == 0, True, [tris[d], lf], [psB])
                p.copy("act", Bt[:], psB[:, 0:2 * NT * 4].rearrange("p (d c h) -> p d c h", d=2, c=NT), [psB], [Bt])
                p.tt("dve", at[:], it[:], Bt[:], ALU.subtract, [it, Bt], [at])
                for tvt in vrot.tiles:
                    p.memset("dve", tvt[:], 1.0, [tvt])
                for h in range(4):
                    qlo, qhi, klo, khi = (qkrot.next() for _ in range(4))
                    for tl, idx in ((qlo, 12 + 2 * h), (qhi, 13 + 2 * h), (klo, 20 + 2 * h), (khi, 21 + 2 * h)):
                        p.dma("sp", tl[:], fmq[idx], [R["fmq"]], [tl], tl)
                    vaug = vrot.next()
                    p.dma("sp", vaug[:, :, 0:DH], tmv.rearrange("(c p) n -> p c n", p=128)[:, :, h * DH:(h + 1) * DH],
                          [R["tmv"]], [vaug], vaug)

                    def fin_ml(ti, j, s8, h=h):
                        p.stt("dve", hsum[:, j, :], hsum[:, j, :], s8[:, 0:1], ngb[:, h * DH:(h + 1) * DH], ALU.mult, ALU.mult,
                              [hsum, s8, ngb], [hsum])
                        p.tt("dve", ytm[:, ti, h * DH:(h + 1) * DH], hsum[:, j, :], osig[:, ti, h * DH:(h + 1) * DH], ALU.mult,
                             [hsum, osig], [ytm_t[ti]])
                    attn_core("ml", h, qlo, qhi, klo, khi, vaug, DH + 1, DH ** -0.5, fin_ml)
            p.barrier()
            if stage <= 2:
                return
            with p.scope() as stk:
                lg = p.sb("lg", [128, 8], F32, stk)
                gsl = p.sb("gsl", [128, NT, RTW], BF16, stk)
                vrot = Rot([p.sb("vrt%d" % i, [128, NT, DH], BF16, stk) for i in range(2)])
                rqk = [None] * 4
                p.dma("sp", lg[:], rt_ld[l:l + 1, :].partition_broadcast(128), [], [lg], lg)
                p.dma("sp", gsl[:], tmg.rearrange("(c p) n -> p c n", p=128), [R["tmg"]], [gsl], gsl)
                p.act(lg[:], lg[:], AF.Exp, [lg], [lg])
                p.ts("dve", lg[:], lg[:], -1.0, None, ALU.mult, reads=[lg], writes=[lg])
                for d in range(2):
                    for h in range(4):
                        p.ts("dve", Bt[:, d, :, h], vpos[:, d, :], lg[:, d * 4 + h:d * 4 + h + 1], None, ALU.mult,
                             reads=[vpos, lg], writes=[Bt])
                p.ts("dve", at[:], Bt[:], -1.0, None, ALU.mult, reads=[Bt], writes=[at])
                for h in range(4):
                    klo, khi = qkrot.next(), qkrot.next()
                    if h % 2 == 0:
                        rqk = [qkrot.next() for _ in range(4)]
                        for i, idx in enumerate((28 + h, 29 + h, 32 + h, 33 + h)):
                            p.dma("sp", rqk[i][:], fmq[idx], [R["fmq"]], [rqk[i]], rqk[i])
                    qA, qB, kA, kB = rqk
                    p.ts("dve", klo[:], kA[:], hmask[:, h:h + 1], None, ALU.mult, reads=[kA, hmask], writes=[klo])
                    p.ts("dve", khi[:], kB[:], hmask[:, h:h + 1], None, ALU.mult, reads=[kB, hmask], writes=[khi])
                    vt = vrot.next()
                    p.dma("sp", vt[:], tmv.rearrange("(c p) n -> p c n", p=128)[:, :, MLW + h * DH:MLW + (h + 1) * DH],
                          [R["tmv"]], [vt], vt)

                    def fin_rt(ti, j, s8, h=h):
                        p.stt("dve", ytm[:, ti, MLW + h * DH:MLW + (h + 1) * DH], hsum[:, j, :], s8[:, 0:1],
                              gsl[:, ti, h * DH:(h + 1) * DH], ALU.mult, ALU.mult, [hsum, s8, gsl], [ytm_t[ti]])
                    attn_core("rt", h, qA, qB, klo, khi, vt, DH, 96 ** -0.5, fin_rt)
            p.barrier()

    negpi_t = p.sb("negpi_t", [128, 1], F32)
    p.memset("dve", negpi_t[:], -math.pi, [negpi_t])

    def hyena(l, seq):
        n = L if seq == "x" else LC
        ndc = n // 128
        col0 = 256 if seq == "x" else 0
        dft = dftx_in if seq == "x" else dftc_in
        z_in = zx_in if seq == "x" else zc_in
        win_in = winx_in if seq == "x" else winc_in
        hw_in = hwx_in if seq == "x" else hwc_in
        tw = min(512, n)
        with p.scope() as ostk:
            Pre = p.sb("Pre", [128, ndc, 512], BF16, ostk)
            nPim = p.sb("nPim", [128, ndc, 512], BF16, ostk)
            hbias = p.sb("hbias", [128, 4], F32, ostk)
            p.dma("sp", hbias[:], hy_bias[l], [], [hbias], hbias)
            with p.scope() as bstk:
                hsum_b = p.sb("hsum_b", [128, ndc, 512], BF16, bstk)
                hdif_b = p.sb("hdif_b", [128, ndc, 512], BF16, bstk)
                invn = p.sb("invn", [128, 512], F32, bstk)
                hwt = p.sb("hwt", [128, ndc, 2], F32, bstk)
                hwn = p.sb("hwn", [128, ndc], F32, bstk)
                p.dma("sp", hwt[:], hw_in, [], [hwt], hwt)
                p.ts("dve", hwn[:], hwt[:, :, 0], -1.0, None, ALU.mult, reads=[hwt], writes=[hwn])
                with p.scope() as stk:
                    w1 = p.sb("hw1", [33, 64], F32, stk)
                    w2_ = p.sb("hw2", [64, 64], F32, stk)
                    w3 = p.sb("hw3", [64, 1024], F32, stk)
                    hv = p.sb("hv", [64, 3], F32, stk)
                    frb = p.sb("frb", [64, 2], F32, stk)
                    zT = p.sb("zT", [33, n], F32, stk)
                    hd1 = p.sb("hd1", [64, n], F32, stk)
                    hd2 = p.sb("hd2", [64, n], F32, stk)
                    arg = p.sb("arg", [64, 512], F32, stk)
                    argi = p.sb("argi", [64, 512], I32, stk)
                    argf = p.sb("argf", [64, 512], F32, stk)
                    winr = Rot([p.sb("win%d" % i, [128, 512], F32, stk) for i in range(2)])
                    hfw = p.sb("hfw", [128, 512], F32, stk)
                    hbw = p.sb("hbw", [128, 512], F32, stk)
                    absr = Rot([p.sb("abs%d" % i, [128, 512], BF16, stk) for i in range(2)])
                    p.dma("sp", w1[:], hy_w1[l], [], [w1], w1)
                    p.dma("sp", w2_[:], hy_w2[l], [], [w2_], w2_)
                    p.dma("sp", w3[:], hy_w3[l], [], [w3], w3)
                    p.dma("sp", hv[:], hy_vec[l], [], [hv], hv)
                    p.dma("sp", zT[:], z_in, [], [zT], zT)
                    p.ts("dve", frb[:, 0:1], hv[:, 0:1], hv[:, 2:3], None, ALU.mult, reads=[hv], writes=[frb])
                    p.ts("dve", frb[:, 1:2], hv[:, 1:2], hv[:, 2:3], None, ALU.mult, reads=[hv], writes=[frb])

                    def mlp_layer(wt_, kdim, src, dst, bcol):
                        for c0 in range(0, n, 512):
                            cw_ = min(512, n - c0)
                            ps = PS[0]
                            p.mm(ps[0:64, 0:cw_], wt_[0:kdim, 0:64], src[0:kdim, c0:c0 + cw_], True, True, [wt_, src], [ps])
                            p.ts("dve", arg[:, 0:cw_], ps[0:64, 0:cw_], hv[:, 2:3], frb[:, bcol:bcol + 1], ALU.mult, op1=ALU.add,
                                 reads=[ps, hv, frb], writes=[arg])
                            p.ts("dve", arg[:, 0:cw_], arg[:, 0:cw_], 1.0 / (2 * math.pi), 64.5, ALU.mult, op1=ALU.add, reads=[arg], writes=[arg])
                            p.copy("dve", argi[:, 0:cw_], arg[:, 0:cw_], [arg], [argi])
                            p.copy("dve", argf[:, 0:cw_], argi[:, 0:cw_], [argi], [argf])
                            p.tt("dve", arg[:, 0:cw_], arg[:, 0:cw_], argf[:, 0:cw_], ALU.subtract, [arg, argf], [arg])
                            p.ts("dve", argf[:, 0:cw_], arg[:, 0:cw_], 0.0, None, ALU.is_lt, reads=[arg], writes=[argf])
                            p.tt("dve", arg[:, 0:cw_], arg[:, 0:cw_], argf[:, 0:cw_], ALU.add, [arg, argf], [arg])
                            p.act(dst[:, c0:c0 + cw_], arg[:, 0:cw_], AF.Sin, [arg], [dst], scale=2 * math.pi, bias=negpi_t[0:64, 0:1])
                    mlp_layer(w1, 33, zT, hd1, 0)
                    mlp_layer(w2_, 64, hd1, hd2, 1)
                    psL1 = PS[6]
                    for dc in range(ndc):
                        wn_ = winr.next()
                        p.dma("sp", wn_[:], win_in[dc * 128:(dc + 1) * 128, :], [], [wn_], wn_)
                        for half, dstw in ((0, hfw), (1, hbw)):
                            ps = PS[1 + half]
                            p.mm(ps[:, :], hd2[0:64, dc * 128:(dc + 1) * 128], w3[0:64, half * 512:(half + 1) * 512], True, True, [hd2, w3], [ps])
                            p.tt("dve", dstw[:], ps[:, :], wn_[:], ALU.mult, [ps, wn_], [dstw])
                        if dc == 0:
                            p.memset("dve", hbw[0:1, :], 0.0, [hbw])
                        p.tt("dve", hsum_b[:, dc, :], hfw[:], hbw[:], ALU.add, [hfw, hbw], [hsum_b])
                        p.tt("dve", hdif_b[:, dc, :], hfw[:], hbw[:], ALU.subtract, [hfw, hbw], [hdif_b])
                        for half, srcw in ((0, hfw), (1, hbw)):
                            ab = absr.next()
                            p.stt("dve", ab[:], srcw[:], -1.0, srcw[:], ALU.mult, ALU.max, [srcw], [ab])
                            p.mm(psL1[:, :], ones_b[:], ab[:], dc == 0 and half == 0, dc == ndc - 1 and half == 1, [ones_b, ab], [psL1])
                    p.op("dve", lambda hh: hh.reciprocal(invn[:], psL1[:, :]), [psL1], [invn])
                p.barrier()
                with p.scope() as stk:
                    zt = p.sb("zt", [128, ndc, 512], BF16, stk)
                    zjr = Rot([p.sb("zj%d" % i, [128, n], BF16, stk) for i in range(2)])
                    crot = Rot([p.sb("Cf%d" % i, [128, ndc, 128], BF16, stk) for i in range(2)])
                    srot = Rot([p.sb("Sf%d" % i, [128, ndc, 128], BF16, stk) for i in range(2)])
                    FKre = p.sb("FKre", [128, 512], F32, stk)
                    FKim = p.sb("FKim", [128, 512], F32, stk)
                    t1 = p.sb("t1", [128, 512], F32, stk)
                    t2 = p.sb("t2", [128, 512], F32, stk)
                    for j in range(4):
                        zj = zjr.next()
                        p.dma("sp", zj[:], hyz[j][:, col0:col0 + n], [R["hyz"]], [zj], zj)
                        for d0 in range(0, ndc, 4):
                            nb = min(4, ndc - d0)
                            for b in range(nb):
                                p.tr(ps_bf[:, b * 128:(b + 1) * 128], zj[:, (d0 + b) * 128:(d0 + b + 1) * 128], ident_b[:], [zj, ident_b], [PS[7]])
                            p.copy("act", zt[:, d0:d0 + nb, j * 128:(j + 1) * 128],
                                   ps_bf[:, 0:nb * 128].rearrange("p (k t) -> p k t", k=nb), [PS[7]], [zt])
                    dC = dft[0].rearrange("(dc p) f -> p dc f", p=128)
                    dS = dft[1].rearrange("(dc p) f -> p dc f", p=128)
                    for fc in range(ndc):
                        Cf, Sf = crot.next(), srot.next()
                        p.dma("pool", Cf[:], dC[:, :, fc * 128:(fc + 1) * 128], [], [Cf], Cf)
                        p.dma("pool", Sf[:], dS[:, :, fc * 128:(fc + 1) * 128], [], [Sf], Sf)
                        combos = [(Cf, hsum_b), (Sf, hsum_b), (Cf, hdif_b), (Sf, hdif_b), (Cf, zt), (Sf, zt)]
                        for ci, (tb, rh) in enumerate(combos):
                            for dc in range(ndc):
                                p.mm(PS[ci][:, :], tb[:, dc, :], rh[:, dc, :], dc == 0, dc == ndc - 1, [tb, rh], [PS[ci]])
                        A1, A2, A3, A4, Zc, Zs = PS[0], PS[1], PS[2], PS[3], PS[4], PS[5]
                        p.ts("dve", FKre[:], A1[:, :], hwt[:, fc, 0:1], None, ALU.mult, reads=[A1, hwt], writes=[FKre])
                        p.stt("dve", FKre[:], A2[:, :], hwt[:, fc, 1:2], FKre[:], ALU.mult, ALU.add, [A2, hwt, FKre], [FKre])
                        p.tt("dve", FKre[:], FKre[:], invn[:], ALU.mult, [FKre, invn], [FKre])
                        p.ts("dve", FKim[:], A3[:, :], hwt[:, fc, 1:2], None, ALU.mult, reads=[A3, hwt], writes=[FKim])
                        p.stt("dve", FKim[:], A4[:, :], hwn[:, fc:fc + 1], FKim[:], ALU.mult, ALU.add, [A4, hwn, FKim], [FKim])
                        p.tt("dve", FKim[:], FKim[:], invn[:], ALU.mult, [FKim, invn], [FKim])
                        p.tt("dve", t1[:], Zc[:, :], FKre[:], ALU.mult, [Zc, FKre], [t1])
                        p.tt("dve", t2[:], Zs[:, :], FKim[:], ALU.mult, [Zs, FKim], [t2])
                        p.tt("dve", Pre[:, fc, :], t1[:], t2[:], ALU.add, [t1, t2], [Pre])
                        p.tt("dve", t1[:], Zs[:, :], FKre[:], ALU.mult, [Zs, FKre], [t1])
                        p.tt("dve", t2[:], Zc[:, :], FKim[:], ALU.mult, [Zc, FKim], [t2])
                        p.tt("dve", nPim[:, fc, :], t1[:], t2[:], ALU.subtract, [t1, t2], [nPim])
                p.barrier()
            p.barrier()
            with p.scope() as stk:
                ctr = Rot([p.sb("Ct%d" % i, [128, ndc, tw], BF16, stk) for i in range(2)])
                str_ = Rot([p.sb("St%d" % i, [128, ndc, tw], BF16, stk) for i in range(2)])
                zr = Rot([p.sb("zti%d" % i, [128, tw], BF16, stk) for i in range(2)])
                xr = Rot([p.sb("x0i%d" % i, [128, tw], BF16, stk) for i in range(2)])
                t1r = Rot([p.sb("iv%d" % i, [128, tw], F32, stk) for i in range(2)])
                psr = Rot([PS[0], PS[1], PS[2], PS[3]])
                dCt = dft[0].rearrange("(fc p) t -> p fc t", p=128)
                dSt = dft[1].rearrange("(fc p) t -> p fc t", p=128)
                for tg in range(n // tw):
                    Ct, St = ctr.next(), str_.next()
                    p.dma("pool", Ct[:], dCt[:, :, tg * tw:(tg + 1) * tw], [], [Ct], Ct)
                    p.dma("pool", St[:], dSt[:, :, tg * tw:(tg + 1) * tw], [], [St], St)
                    tc0 = col0 + tg * tw
                    for j in range(4):
                        ps = psr.next()
                        for fc in range(ndc):
                            p.mm(ps[:, 0:tw], Pre[:, fc, j * 128:(j + 1) * 128], Ct[:, fc, :], fc == 0, False, [Pre, Ct], [ps])
                            p.mm(ps[:, 0:tw], nPim[:, fc, j * 128:(j + 1) * 128], St[:, fc, :], False, fc == ndc - 1, [nPim, St], [ps])
                        zi, xi, iv = zr.next(), xr.next(), t1r.next()
                        p.dma("sp", zi[:], hyz[j][:, tc0:tc0 + tw], [R["hyz"]], [zi], zi)
                        p.dma("sp", xi[:], fmq[3 * j][:, tc0:tc0 + tw], [R["fmq"]], [xi], xi)
                        p.act(iv[:], ps[:, 0:tw], AF.Copy, [ps], [iv], scale=1.0 / n)
                        p.stt("dve", iv[:], zi[:], hbias[:, j:j + 1], iv[:], ALU.mult, ALU.add, [zi, hbias, iv], [iv])
                        p.tt("dve", yhT[:, j, tc0:tc0 + tw], iv[:], xi[:], ALU.mult, [iv, xi], [yh_res])
            p.barrier()

    def fill_mod(A, sh, l, r, j_shift, j_scale, g_row):
        bcast_row(A, modv[l, r:r + 1, j_scale * D:(j_scale + 1) * D])
        bcast_row(sh, g_row)
        p.stt("dve", A[:], A[:], 1.0, sh[:], ALU.add, ALU.mult, [A, sh], [A])
        bcast_row(sh, modv[l, r:r + 1, j_shift * D:(j_shift + 1) * D])

    def outproj(l):
        with p.scope() as stk:
            wo = p.sb("wo", [128, 16, D], BF16, stk)
            wov = w_out[l].rearrange("(kc p) n -> p kc n", p=128)
            for cg in range(4):
                p.dma("pool", wo[:, :, cg * 512:(cg + 1) * 512], wov[:, :, cg * 512:(cg + 1) * 512], [], [wo], wo)
            g_x = p.sb("g_x", [128, D], F32, stk)
            bcast_row(g_x, modv[l, 0:1, 2 * D:3 * D])
            g_c = None
            if l == 0:
                g_c = p.sb("g_c", [128, D], F32, stk)
                bcast_row(g_c, modv[l, 1:2, 2 * D:3 * D])
            xrot = Rot([p.sb("xo%d" % i, [128, D], F32, stk) for i in range(2)])
            yrot = Rot([p.sb("yTt%d" % i, [128, 12, 128], BF16, stk) for i in range(2)])
            trot = Rot([p.sb("to%d" % i, [128, 512], F32, stk) for i in range(2)])
            psrot = Rot([PS[0], PS[1], PS[2], PS[3]])
            for i in (range(NT) if l == 0 else range(2, NT)):
                xt = xrot.next()
                src, rd = x_src(l, i)
                p.dma("sp", xt[:], src, rd, [xt], xt)
                yt = yrot.next()
                for b3 in range(3):
                    for b in range(4):
                        k = b3 * 4 + b
                        p.tr(ps_bf[:, b * 128:(b + 1) * 128], ytm[:, i, k * 128:(k + 1) * 128], ident_b[:], [ytm_t[i], ident_b], [PS[7]])
                    p.copy("act", yt[:, b3 * 4:(b3 + 1) * 4, :], ps_bf[:, 0:512].rearrange("p (k t) -> p k t", k=4), [PS[7]], [yt])
                gate = g_c if i < 2 else g_x
                for cg in range(4):
                    ps = psrot.next()
                    for k in range(16):
                        if k < 4:
                            lhsT, rd2 = yhT[:, k, i * 128:(i + 1) * 128], yh_res
                        else:
                            lhsT, rd2 = yt[:, k - 4, :], yt
                        p.mm(ps[:, :], lhsT, wo[:, k, cg * 512:(cg + 1) * 512], k == 0, k == 15, [rd2, wo], [ps])
                    tmp = trot.next()
                    p.tt("dve", tmp[:], ps[:, :], gate[:, cg * 512:(cg + 1) * 512], ALU.mult, [ps, gate], [tmp])
                    p.tt("dve", xt[:, cg * 512:(cg + 1) * 512], xt[:, cg * 512:(cg + 1) * 512], tmp[:], ALU.add, [xt, tmp], [xt])
                p.dma("sp", xs[i * 128:(i + 1) * 128, :], xt[:], [xt], [xs_t[i]], xt)

    hidden = bufA.t[:, 0:64 * 512].rearrange("p (j t) -> p j t", j=64)

    def ffn(l):
        final = (l == DEPTH - 1)
        groups = GROUPS if l == 0 else GROUPS[1:]
        with p.scope() as stk:
            A = p.sb("A2", [128, D], F32, stk)
            sh = p.sb("sh2", [128, D], F32, stk)
            gt = p.sb("g2", [128, D], F32, stk)
            xg_t = [p.sb("xg%d" % j, [128, D], F32, stk) for j in range(4)]
            h2T = p.sb("h2T", [128, 16, 512], BF16, stk)
            wrot = Rot([p.sb("wf%d" % i, [128, 16, 512], BF16, stk) for i in range(2)])
            scratch = p.sb("scr", [128, D], F32, stk)
            hb = p.sb("hb2", [128, D], BF16, stk)
            rrot = Rot([p.sb("rr%d" % i, [128, 512], BF16, stk) for i in range(2)])
            trot = Rot([p.sb("tf%d" % i, [128, 512], F32, stk) for i in range(2)])
            ssr = Rot([p.sb("ssf%d" % i, [128, 1], F32, stk) for i in range(2)])
            if final:
                fg = p.sb("fg", [128, D], F32, stk)
                p.dma("sp", fg[:], final_g.partition_broadcast(128), [], [fg], fg)
            w1v = w_ff1[l].rearrange("(kc p) n -> p kc n", p=128)
            ps1 = Rot([PS[4], PS[5]])
            cur_r = None
            for (t0, w) in groups:
                r = 1 if t0 == 0 else 0
                if r != cur_r:
                    fill_mod(A, sh, l, r, 3, 4, norm2_g[l:l + 1, :])
                    bcast_row(gt, modv[l, r:r + 1, 5 * D:6 * D])
                    cur_r = r
                N = w * 128
                for j in range(w):
                    i = t0 + j
                    p.dma("sp", xg_t[j][:], xs[i * 128:(i + 1) * 128, :], [xs_t[i]], [xg_t[j]], xg_t[j])
                    ss = ssr.next()
                    p.memset("dve", ss[:], 0.0, [ss])
                    p.act(hb[:], xg_t[j][:], AF.Square, [xg_t[j]], [hb, ss], accum_out=ss[:, 0:1])
                    rstd_from_sumsq(ss[:, 0:1], ss[:, 0:1], D, [ss])
                    p.stt("dve", scratch[:], xg_t[j][:], ss[:, 0:1], A[:], ALU.mult, ALU.mult, [xg_t[j], ss, A], [scratch])
                    p.tt("dve", hb[:], scratch[:], sh[:], ALU.add, [scratch, sh], [hb])
                    transpose_to(hb, h2T, h2T, j * 128)
                for wi in range(16):
                    wt = wrot.next()
                    p.dma("pool", wt[:], w1v[:, :, wi * 512:(wi + 1) * 512], [], [wt], wt)
                    for jj in range(4):
                        ps = ps1.next()
                        for k in range(16):
                            p.mm(ps[:, 0:N], wt[:, k, jj * 128:(jj + 1) * 128], h2T[:, k, 0:N], k == 0, k == 15, [wt, h2T], [ps])
                        rr = rrot.next()
                        p.act(rr[:, 0:N], ps[:, 0:N], AF.Relu, [ps], [rr])
                        p.tt("dve", hidden[:, wi * 4 + jj, 0:N], rr[:, 0:N], rr[:, 0:N], ALU.mult, [rr], [bufA])
                for cg in range(4):
                    for kq in range(4):
                        wt = wrot.next()
                        p.dma("pool", wt[:], w_ff2[l, kq * 2048:(kq + 1) * 2048, cg * 512:(cg + 1) * 512].rearrange("(kc p) n -> p kc n", p=128),
                              [], [wt], wt)
                        for j in range(w):
                            for k in range(16):
                                p.mm(PS[j][:, :], hidden[:, kq * 16 + k, j * 128:(j + 1) * 128], wt[:, k, :],
                                     kq == 0 and k == 0, kq == 3 and k == 15, [bufA, wt], [PS[j]])
                    for j in range(w):
                        tmp = trot.next()
                        p.tt("dve", tmp[:], PS[j][:, :], gt[:, cg * 512:(cg + 1) * 512], ALU.mult, [PS[j], gt], [tmp])
                        p.tt("dve", xg_t[j][:, cg * 512:(cg + 1) * 512], xg_t[j][:, cg * 512:(cg + 1) * 512], tmp[:], ALU.add,
                             [xg_t[j], tmp], [xg_t[j]])
                for j in range(w):
                    i = t0 + j
                    if not final:
                        p.dma("sp", xs[i * 128:(i + 1) * 128, :], xg_t[j][:], [xg_t[j]], [xs_t[i]], xg_t[j])
                    else:
                        ss = ssr.next()
                        p.memset("dve", ss[:], 0.0, [ss])
                        p.act(hb[:], xg_t[j][:], AF.Square, [xg_t[j]], [hb, ss], accum_out=ss[:, 0:1])
                        rstd_from_sumsq(ss[:, 0:1], ss[:, 0:1], D, [ss])
                        p.stt("dve", scratch[:], xg_t[j][:], ss[:, 0:1], fg[:], ALU.mult, ALU.mult, [xg_t[j], ss, fg], [scratch])
                        p.dma("sp", yout[(i - 2) * 128:(i - 1) * 128, :], scratch[:], [scratch], [R["yout"]], scratch)

    if debug:
        dbg_ss = nc.dram_tensor("dbg_ss", [128, 1], F32, kind="ExternalOutput").ap()
        dbg_h = nc.dram_tensor("dbg_h", [128, D], BF16, kind="ExternalOutput").ap()
        dbg_x = nc.dram_tensor("dbg_x", [128, D], F32, kind="ExternalOutput").ap()
    if layer_front(0) == "stop":
        return nc, p
    if stage <= 1:
        p.finish()
        return nc, p
    mixers(0)
    if stage >= 4:
        hyena(0, "x")
        hyena(0, "c")
    if debug:
        dbg_yh = nc.dram_tensor("dbg_yh", [128, 4 * NTOK], BF16, kind="ExternalOutput").ap()
        p.dma("sp", dbg_yh, bufA.t[:, NT * 1536:16 * NTOK], [yh_res], [R["yout"]], bufA)
        dbg_ytm = nc.dram_tensor("dbg_ytm", [128, NT * 1536], BF16, kind="ExternalOutput").ap()
        p.dma("sp", dbg_ytm, bufA.t[:, 0:NT * 1536], ytm_t, [R["yout"]], bufA)
    if stage <= 4:
        p.finish()
        return nc, p
    outproj(0)
    if stage <= 5:
        p.finish()
        return nc, p
    ffn(0)
    if stage <= 6:
        p.finish()
        return nc, p
    layer_front(1)
    mixers(1)
    hyena(1, "x")
    outproj(1)
    ffn(1)
    p.finish()
    return nc, p


_CONST_CACHE = {}


def const_inputs():
    if _CONST_CACHE:
        return _CONST_CACHE
    ident, tri_f, tri_b, mask, vpos, hmask = attn_consts()
    cosT, sinT = rope_tables()
    zx, winx = hyena_consts(L)
    zc, winc = hyena_consts(LC)
    cx, sx, cwx, swx = dft_tables(L)
    cc, sc, cwc, swc = dft_tables(LC)
    hwx = np.stack([cwx.reshape(16, 128).T, swx.reshape(16, 128).T], axis=-1)
    hwc = np.stack([cwc.reshape(2, 128).T, swc.reshape(2, 128).T], axis=-1)
    _CONST_CACHE.update(dict(
        ident=ident, tri=np.stack([tri_f, tri_b]), mask=mask, vpos=vpos, hmask=hmask,
        rope=np.stack([cosT, sinT]), zx=zx, zc=zc, winx=winx, winc=winc,
        dftx=np.stack([cx, sx]), dftc=np.stack([cc, sc]),
        hwx=np.ascontiguousarray(hwx, dtype=np.float32), hwc=np.ascontiguousarray(hwc, dtype=np.float32)))
    for k, v in _CONST_CACHE.items():
        _CONST_CACHE[k] = np.ascontiguousarray(v, dtype=np.float32)
    return _CONST_CACHE


def prep_inputs(x, c, ctx, c_ctx, norm1_g, norm2_g, w_mod, b_mod, w_in, hy_conv_w, hy_f_w1, hy_f_b1,
                hy_f_w2, hy_f_b2, hy_f_w3, hy_f_freq, hy_bias, ml_conv_w, ml_gate_b, ml_norm_g,
                rt_log_decay, w_out, w_ff1, w_ff2, final_g, cores=range(8)):
    f = lambda a: np.ascontiguousarray(np.asarray(a), dtype=np.float32)
    shared = dict(const_inputs())
    shared.update(
        norm1_g=f(norm1_g), norm2_g=f(norm2_g), final_g=f(final_g).reshape(1, D),
        w_mod=f(w_mod), b_mod=f(b_mod),
        w2=np.stack([build_w2(f(w_in[l])) for l in range(DEPTH)]),
        convw=np.stack([build_convw(f(hy_conv_w[l]), f(ml_conv_w[l])) for l in range(DEPTH)]),
        w_out=f(w_out), w_ff1=f(w_ff1), w_ff2=f(w_ff2),
        hy_w1=f(hy_f_w1), hy_w2=f(hy_f_w2), hy_w3=f(hy_f_w3),
        hy_vec=np.ascontiguousarray(np.stack([f(hy_f_b1), f(hy_f_b2), f(hy_f_freq)], axis=-1)),
        hy_bias=np.ascontiguousarray(f(hy_bias).reshape(DEPTH, 4, 128).transpose(0, 2, 1)),
        gateb_t=np.ascontiguousarray(np.tile(f(ml_gate_b), (1, NT))), ml_norm_g=f(ml_norm_g), rt_ld=f(rt_log_decay).reshape(DEPTH, 8),
    )
    x = f(x)
    ctx = f(ctx)
    c = f(c)
    c_ctx = f(c_ctx)
    maps = []
    for b in cores:
        m = dict(shared)
        m["xin"] = np.concatenate([ctx[b], x[b]], axis=0)
        m["c2"] = np.ascontiguousarray(np.stack([c[b].reshape(16, 128).T, c_ctx.reshape(16, 128).T], axis=-1))
        maps.append(m)
    return maps


_PROG_CACHE = {}


def kernel(**inputs):
    maps = prep_inputs(**inputs)
    nc, p = build_program()
    res = run_bass_kernel_spmd(nc, maps, core_ids=list(range(8)))
    return np.stack([np.asarray(r["yout"], dtype=np.float32) for r in res.results], axis=0)
```

```python
import contextlib
import math
import numpy as np
import concourse.bass as bass
import concourse.mybir as mybir
from concourse.bass_utils import run_bass_kernel_spmd

F32 = mybir.dt.float32
BF16 = mybir.dt.bfloat16
I32 = mybir.dt.int32
ALU = mybir.AluOpType
AF = mybir.ActivationFunctionType
AX = mybir.AxisListType

D = 2048
L = 2048
LC = 256
NT = 18
NTOK = NT * 128
DEPTH = 2
HYW = 512
MLW = 768
RTW = 768
DH = 192
NEG = -30000.0
EPS = 1e-6
NFM = 36
TMW = 3344
W2C = NFM * 128 + TMW
TM_GROUPS = [(0, 512), (512, 512), (1024, 512), (1536, 512), (2048, 512), (2560, 512), (3072, 256), (3328, 16)]
GROUPS = [(0, 2), (2, 4), (6, 4), (10, 4), (14, 4)]


class Res:
    __slots__ = ("name", "w", "r", "dsem", "dcnt", "t")

    def __init__(self, name, t=None):
        self.name = name
        self.w = None
        self.r = {}
        self.dsem = None
        self.dcnt = 0
        self.t = t

    def __getitem__(self, idx):
        return self.t[idx]


class Prog:
    ENG = ("pe", "dve", "act", "pool", "sp")

    def __init__(self, nc):
        self.nc = nc
        self.stack = contextlib.ExitStack()
        self.h = {"pe": nc.tensor, "dve": nc.vector, "act": nc.scalar, "pool": nc.gpsimd, "sp": nc.sync}
        self.sems = {}
        for k in self.ENG:
            self.sems[k] = self.stack.enter_context(nc.semaphore("sem_" + k))
        self.cnt = {k: 0 for k in self.ENG}
        self.seen = {k: {} for k in self.ENG}
        self.ninst = {k: 0 for k in self.ENG}
        self.ndsem = 0
        self.out_events = {}
        self.dma_tot = {}
        self.dsem_pool = []

    def sb(self, name, shape, dtype, stk=None):
        self.ndsem_names = getattr(self, "ndsem_names", 0) + 1
        name = "sb%d_%s" % (self.ndsem_names, name)
        t = (stk.es if stk is not None else self.stack).enter_context(self.nc.sbuf_tensor(name, list(shape), dtype))
        r = Res(name, t)
        if stk is not None:
            stk.res.append(r)
        return r

    def scope(self):
        return Scope(self)

    def ps(self, name, shape, dtype=F32):
        t = self.stack.enter_context(self.nc.psum_tensor(name, list(shape), dtype))
        return Res(name, t)

    def res(self, name):
        return Res(name)

    def _deps(self, reads, writes):
        deps = {}
        for r in reads:
            if r.w is not None:
                k, v = r.w
                if deps.get(k, 0) < v:
                    deps[k] = v
        for w in writes:
            if w.w is not None:
                k, v = w.w
                if deps.get(k, 0) < v:
                    deps[k] = v
            for k, v in w.r.items():
                if deps.get(k, 0) < v:
                    deps[k] = v
        return deps

    def _waits(self, e, deps, is_mm=False):
        waits = []
        seen = self.seen[e]
        for k, v in deps.items():
            if is_mm and k == "pe":
                continue
            if seen.get(k, 0) >= v:
                continue
            seen[k] = v
            waits.append((k, v))
        return waits

    def _mark(self, ev, reads, writes):
        k, v = ev
        for r in reads:
            if r.r.get(k, 0) < v:
                r.r[k] = v
        for w in writes:
            w.w = ev
            w.r = {}

    def _emit(self, e, fn, waits, inc):
        h = self.h[e]
        for k, v in waits:
            h.wait_ge(self.sems[k], v)
            self.ninst[e] += 1
        if fn is None:
            return
        ins = fn(h)
        self.ninst[e] += 1
        if inc is not None:
            ins.then_inc(self.sems[inc[0]], inc[1])

    def op(self, e, fn, reads=(), writes=(), is_mm=False):
        deps = self._deps(reads, writes)
        waits = self._waits(e, deps, is_mm)
        self.cnt[e] += 1
        ev = (e, self.cnt[e])
        self._emit(e, fn, waits, (e, 1))
        self._mark(ev, reads, writes)
        return ev

    def dma(self, q, out_ap, in_ap, reads, writes, semres, out_final=False, **kw):
        deps = self._deps(reads, writes)
        waits = self._waits(q, deps)
        if semres.dsem is None:
            if self.dsem_pool:
                semres.dsem, semres.dcnt = self.dsem_pool.pop()
            else:
                semres.dsem = self.stack.enter_context(self.nc.semaphore("dsem%d" % self.ndsem))
                self.ndsem += 1
        semres.dcnt += 16
        key = ("d", semres.dsem.num)
        self.sems[key] = semres.dsem
        ev = (key, semres.dcnt)
        self.dma_tot[key] = semres.dcnt
        self._emit(q, lambda h: h.dma_start(out=out_ap, in_=in_ap, **kw), waits, (key, 16))
        self._mark(ev, reads, writes)
        if out_final:
            self.out_events[key] = semres.dcnt
        return ev

    def barrier(self):
        allev = {k: self.cnt[k] for k in self.ENG if self.cnt[k] > 0}
        allev.update(self.dma_tot)
        for e in self.ENG:
            waits = self._waits(e, dict(allev))
            self._emit(e, None, waits, None)

    def finish(self):
        allev = {k: self.cnt[k] for k in self.ENG if self.cnt[k] > 0}
        allev.update(self.dma_tot)
        waits = self._waits("sp", allev)
        self._emit("sp", None, waits, None)

    def mm(self, out_ap, lhsT, rhs, start, stop, reads, writes):
        return self.op("pe", lambda h: h.matmul(out_ap, lhsT, rhs, start=start, stop=stop), reads, writes, is_mm=True)

    def tr(self, out_ap, in_ap, ident_ap, reads, writes):
        return self.op("pe", lambda h: h.transpose(out_ap, in_ap, ident_ap), reads, writes, is_mm=True)

    def act(self, out_ap, in_ap, func, reads, writes, bias=None, scale=None, accum_out=None):
        kw = {}
        if bias is not None:
            kw["bias"] = bias
        if scale is not None:
            kw["scale"] = scale
        if accum_out is not None:
            kw["accum_out"] = accum_out
        return self.op("act", lambda h: h.activation(out_ap, in_ap, func, **kw), reads, writes)

    def tt(self, e, out_ap, in0, in1, op, reads, writes):
        return self.op(e, lambda h: h.tensor_tensor(out_ap, in0, in1, op), reads, writes)

    def ts(self, e, out_ap, in0, s1, s2, op0, op1=None, reads=(), writes=()):
        kw = {}
        if op1 is not None:
            kw["op1"] = op1
        return self.op(e, lambda h: h.tensor_scalar(out_ap, in0, s1, s2, op0, **kw), reads, writes)

    def stt(self, e, out_ap, in0, scalar, in1, op0, op1, reads, writes):
        return self.op(e, lambda h: h.scalar_tensor_tensor(out_ap, in0, scalar, in1, op0, op1), reads, writes)

    def copy(self, e, out_ap, in_ap, reads, writes):
        if e == "act":
            return self.op(e, lambda h: h.copy(out_ap, in_ap), reads, writes)
        return self.op(e, lambda h: h.tensor_copy(out_ap, in_ap), reads, writes)

    def memset(self, e, ap, val, writes):
        return self.op(e, lambda h: h.memset(ap, val), (), writes)


class Scope:
    def __init__(self, prog):
        self.p = prog
        self.es = contextlib.ExitStack()
        self.res = []

    def __enter__(self):
        self.es.__enter__()
        return self

    def __exit__(self, *a):
        if a[0] is None:
            self.p.barrier()
            for r in self.res:
                if r.dsem is not None:
                    self.p.dsem_pool.append((r.dsem, r.dcnt))
                    r.dsem = None
        return self.es.__exit__(*a)


class Rot:
    def __init__(self, tiles):
        self.tiles = tiles
        self.i = 0

    def next(self):
        t = self.tiles[self.i % len(self.tiles)]
        self.i += 1
        return t


def fm_src_cols():
    src = np.full((NFM, 128), -1, np.int64)
    for j in range(4):
        src[3 * j] = np.arange(128) + j * 128
        src[3 * j + 1] = 512 + np.arange(128) + j * 128
        src[3 * j + 2] = 1024 + np.arange(128) + j * 128
    for part, base in ((0, 1536), (1, 1536 + 768)):
        for h in range(4):
            c0 = 12 + part * 8 + 2 * h
            src[c0] = base + h * 192 + np.arange(128)
            src[c0 + 1, :64] = base + h * 192 + 128 + np.arange(64)
    rt = 1536 + 3072 + 16
    for part, base in ((0, rt), (1, rt + 384)):
        for h in range(4):
            ca = 28 + 4 * part + 2 * (h // 2)
            lo = (h % 2) * 48
            src[ca, lo:lo + 48] = base + h * 96 + np.arange(48)
            src[ca + 1, lo:lo + 48] = base + h * 96 + 48 + np.arange(48)
    return src


def tm_src_cols():
    rt = 1536 + 3072 + 16
    src = np.full((TMW,), -1, np.int64)
    src[0:768] = 1536 + 1536 + np.arange(768)
    src[768:1536] = rt + 768 + np.arange(768)
    src[1536:2304] = 1536 + 2304 + np.arange(768)
    src[2560:3328] = rt + 1536 + np.arange(768)
    src[3328:3344] = 1536 + 3072 + np.arange(16)
    return src


def build_w2(w_in_l):
    src = np.concatenate([fm_src_cols().reshape(-1), tm_src_cols()])
    w2 = np.zeros((D, W2C), np.float32)
    ok = src >= 0
    w2[:, ok] = w_in_l[:, src[ok]]
    return w2


def build_convw(hy_conv_l, ml_conv_l):
    src = fm_src_cols()
    cw = np.zeros((128, 28, 3), np.float32)
    for cc in range(28):
        for p in range(128):
            s = src[cc, p]
            if s < 0:
                continue
            if cc < 12:
                cw[p, cc, :] = hy_conv_l[:, s]
            else:
                cw[p, cc, :] = ml_conv_l[:, s - 1536]
    return cw


def dft_tables(n):
    N = 2 * n
    k = np.arange(n, dtype=np.float64) + 0.5
    ang = 2.0 * np.pi * np.outer(k, k) / N
    cw = np.cos(np.pi * k / N)
    sw = np.sin(np.pi * k / N)
    return np.cos(ang).astype(np.float32), np.sin(ang).astype(np.float32), cw.astype(np.float32), sw.astype(np.float32)


def hyena_consts(n):
    f32 = np.float32
    t = np.linspace(0.0, 1.0, n, dtype=f32)[:, None]
    w = (f32(2.0 * math.pi / n)) * np.arange(n, dtype=f32)[:, None]
    bands = np.linspace(1e-4, 15, 16, dtype=f32)[None, :]
    z = np.concatenate([t, np.cos(bands * w), -np.sin(bands * w)], axis=-1).astype(f32)
    deltas = np.abs(np.linspace(math.log(1e-2) / 1.5, math.log(1e-2) / 0.3, HYW, dtype=f32))
    win = np.exp(-t * deltas).astype(f32)
    return np.ascontiguousarray(z.T), win


def rope_tables():
    rows = L // 64
    r = np.repeat(np.arange(rows, dtype=np.float32), 64)
    col = np.tile(np.arange(64, dtype=np.float32), rows)
    nf = 24
    inv = (10000.0 ** (-np.arange(nf, dtype=np.float32) / nf)).astype(np.float32)
    ang = np.concatenate([r[:, None] * inv, col[:, None] * inv], axis=-1)
    cos = np.cos(ang).astype(np.float32).T
    sin = np.sin(ang).astype(np.float32).T
    z = np.zeros((32, L), np.float32)
    return np.concatenate([cos, cos, z]), np.concatenate([sin, sin, z])


def attn_consts():
    ident = np.eye(128, dtype=np.float32)
    r = np.arange(128)
    tri_f = (r[:, None] <= r[None, :]).astype(np.float32)
    tri_b = (r[:, None] >= r[None, :]).astype(np.float32)
    mask = np.zeros((2, 4, 128, 512), np.float32)
    for j in range(4):
        for i in range(4):
            blk = slice(i * 128, (i + 1) * 128)
            if i < j:
                mask[0, j, :, blk] = NEG
            elif i == j:
                mask[0, j, :, blk] = np.where(r[None, :] >= r[:, None], 0.0, NEG)
            if i > j:
                mask[1, j, :, blk] = NEG
            elif i == j:
                mask[1, j, :, blk] = np.where(r[None, :] <= r[:, None], 0.0, NEG)
    vpos = np.zeros((128, 2, NT), np.float32)
    for c in range(NT):
        vpos[:, 0, c] = c * 128 + r + 1
    order_b = [1, 0] + list(range(17, 1, -1))
    for vi, c in enumerate(order_b):
        vpos[:, 1, c] = vi * 128 + (127 - r) + 1
    hmask = np.zeros((128, 4), np.float32)
    for h in range(4):
        hmask[(h % 2) * 48:(h % 2) * 48 + 48, h] = 1.0
    return ident, tri_f, tri_b, mask, vpos, hmask


def build_program(debug=False, stage=99):
    nc = bass.Bass("TRN2", target_bir_lowering=False)
    p = Prog(nc)

    def din(name, shape, dt=F32):
        return nc.dram_tensor(name, list(shape), dt, kind="ExternalInput").ap()

    def dscr(name, shape, dt):
        kind = "ExternalOutput" if debug else "Internal"
        return nc.dram_tensor(name, list(shape), dt, kind=kind).ap()

    xin = din("xin", [NTOK, D])
    c2 = din("c2", [128, 16, 2])
    norm1_g = din("norm1_g", [DEPTH, D])
    norm2_g = din("norm2_g", [DEPTH, D])
    final_g = din("final_g", [1, D])
    w_mod = din("w_mod", [DEPTH, D, 6 * D])
    b_mod = din("b_mod", [DEPTH, 6 * D])
    w2 = din("w2", [DEPTH, D, W2C])
    convw = din("convw", [DEPTH, 128, 28, 3])
    w_out = din("w_out", [DEPTH, D, D])
    w_ff1 = din("w_ff1", [DEPTH, D, 4 * D])
    w_ff2 = din("w_ff2", [DEPTH, 4 * D, D])
    hy_w1 = din("hy_w1", [DEPTH, 33, 64])
    hy_w2 = din("hy_w2", [DEPTH, 64, 64])
    hy_w3 = din("hy_w3", [DEPTH, 64, 1024])
    hy_vec = din("hy_vec", [DEPTH, 64, 3])
    hy_bias = din("hy_bias", [DEPTH, 128, 4])
    gateb_t = din("gateb_t", [DEPTH, NT * 16])
    ml_norm_g = din("ml_norm_g", [DEPTH, 768])
    rt_ld = din("rt_ld", [DEPTH, 8])
    ident_in = din("ident", [128, 128])
    tri_in = din("tri", [2, 128, 128])
    mask_in = din("mask", [2, 4, 128, 512])
    vpos_in = din("vpos", [128, 2, NT])
    hmask_in = din("hmask", [128, 4])
    rope_in = din("rope", [2, 128, L])
    zx_in = din("zx", [33, L])
    zc_in = din("zc", [33, LC])
    winx_in = din("winx", [L, HYW])
    winc_in = din("winc", [LC, HYW])
    dftx_in = din("dftx", [2, L, L])
    dftc_in = din("dftc", [2, LC, LC])
    hwx_in = din("hwx", [128, 16, 2])
    hwc_in = din("hwc", [128, 2, 2])
    yout = nc.dram_tensor("yout", [L, D], F32, kind="ExternalOutput").ap()
    xs = dscr("xs", [NTOK, D], F32)
    modv = dscr("modv", [DEPTH, 2, 6 * D], F32)
    fmq = dscr("fmq", [NFM, 128, NTOK], BF16)
    hyz = dscr("hyz", [4, 128, NTOK], BF16)
    tmv = dscr("tmv", [NTOK, 1536], BF16)
    tmo = dscr("tmo", [NTOK, 1024], BF16)
    tmg = dscr("tmg", [NTOK, 768], BF16)
    tgate = dscr("tgate", [NTOK, 16], F32)
    R = {k: p.res(k) for k in ("xs", "modv", "fmq", "hyz", "tmv", "tmo", "tmg", "tgate", "yout")}
    xs_t = [p.res("xs%d" % i) for i in range(NT)]

    ident_f = p.sb("ident_f", [128, 128], F32)
    ident_b = p.sb("ident_b", [128, 128], BF16)
    ones_f = p.sb("ones_f", [128, 128], F32)
    ones_b = p.sb("ones_b", [128, 128], BF16)
    bufA = p.sb("bufA", [128, 16 * NTOK], BF16)
    hT = bufA.t[:, :].rearrange("p (k t) -> p k t", k=16)
    PS = [p.ps("psb%d" % i, [128, 512]) for i in range(8)]

    p.dma("sp", ident_f[:], ident_in, [], [ident_f], ident_f)
    p.copy("dve", ident_b[:], ident_f[:], [ident_f], [ident_b])
    p.memset("dve", ones_f[:], 1.0, [ones_f])
    p.memset("dve", ones_b[:], 1.0, [ones_b])

    wsem_i = [0]

    def rstd_from_sumsq(dst, src, n, reads_writes):
        p.act(dst, src, AF.Ln, reads_writes, reads_writes, scale=1.0 / n, bias=eps_t[:, 0:1])
        p.act(dst, dst, AF.Exp, reads_writes, reads_writes, scale=-0.5)

    eps_t = p.sb("eps_t", [128, 1], F32)
    p.memset("dve", eps_t[:], EPS, [eps_t])
    one_t = p.sb("one_t", [128, 1], F32)
    p.memset("dve", one_t[:], 1.0, [one_t])

    with p.scope() as stk:
        c2t = p.sb("c2t", [128, 16, 2], F32, stk)
        sT = p.sb("sT", [128, 16, 2], BF16, stk)
        p.dma("sp", c2t[:], c2, [], [c2t], c2t)
        p.act(sT[:], c2t[:], AF.Silu, [c2t], [sT])
        wrot = Rot([p.sb("wm%d" % i, [128, 16, 512], BF16, stk) for i in range(3)])
        brot = Rot([p.sb("bm%d" % i, [2, 512], F32, stk) for i in range(2)])
        orot = Rot([p.sb("om%d" % i, [2, 512], F32, stk) for i in range(2)])
        prot = Rot([PS[0], PS[1]])
        for l in range(DEPTH):
            wv = w_mod[l].rearrange("(kc p) n -> p kc n", p=128)
            for g in range(24):
                wt = wrot.next()
                p.dma("pool", wt[:], wv[:, :, g * 512:(g + 1) * 512], [], [wt], wt)
                bt = brot.next()
                p.dma("sp", bt[:], b_mod[l:l + 1, g * 512:(g + 1) * 512].partition_broadcast(2), [], [bt], bt)
                ps = prot.next()
                for kc in range(16):
                    p.mm(ps[0:2, :], sT[:, kc, :], wt[:, kc, :], kc == 0, kc == 15, [sT, wt], [ps])
                ot = orot.next()
                p.tt("dve", ot[:], ps[0:2, :], bt[:], ALU.add, [ps, bt], [ot])
                p.dma("sp", modv[l, :, g * 512:(g + 1) * 512], ot[:], [ot], [R["modv"]], ot)
    p.barrier()
    if stage == 0:
        p.finish()
        return nc, p

    hT_t = [p.res("hT%d" % i) for i in range(NT)]
    ps_bf = PS[7].t[:, :].bitcast(BF16)

    def bcast_row(dst, row_ap, q="sp"):
        p.dma(q, dst[:], row_ap.partition_broadcast(128), [R["modv"]], [dst], dst)

    def make_mod_tiles(stk, l, r, j_shift, j_scale, g_row, tmp_g, tmp_s, tag):
        A = p.sb("A_%s" % tag, [128, D], F32, stk)
        sh = p.sb("sh_%s" % tag, [128, D], F32, stk)
        bcast_row(tmp_g, g_row)
        bcast_row(tmp_s, modv[l, r:r + 1, j_scale * D:(j_scale + 1) * D])
        p.stt("dve", A[:], tmp_s[:], 1.0, tmp_g[:], ALU.add, ALU.mult, [tmp_s, tmp_g], [A])
        bcast_row(sh, modv[l, r:r + 1, j_shift * D:(j_shift + 1) * D])
        return A, sh

    def norm_mod_tile(xt, A, sh, hb, ss, junk):
        p.memset("dve", ss[:], 0.0, [ss])
        p.act(junk[:], xt[:], AF.Square, [xt], [junk, ss], accum_out=ss[:, 0:1])
        rstd_from_sumsq(ss[:, 0:1], ss[:, 0:1], D, [ss])
        p.stt("dve", xt[:], xt[:], ss[:, 0:1], A[:], ALU.mult, ALU.mult, [xt, ss, A], [xt])
        p.tt("dve", hb[:], xt[:], sh[:], ALU.add, [xt, sh], [hb])

    def transpose_to(hb, dst_view, dst_res, tcol0):
        for q4 in range(4):
            for j in range(4):
                k = q4 * 4 + j
                p.tr(ps_bf[:, j * 128:(j + 1) * 128], hb[:, k * 128:(k + 1) * 128], ident_b[:], [hb, ident_b], [PS[7]])
            src = ps_bf[:, 0:512].rearrange("p (k t) -> p k t", k=4)
            p.copy("act", dst_view[:, q4 * 4:(q4 + 1) * 4, tcol0:tcol0 + 128], src, [PS[7]], [dst_res])

    def x_src(l, i):
        if l == 0:
            return xin[i * 128:(i + 1) * 128, :], []
        return xs[i * 128:(i + 1) * 128, :], [xs_t[i]]

    def layer_front(l):
        with p.scope() as stk:
            tmp_g = p.sb("tmp_g", [128, D], F32, stk)
            tmp_s = p.sb("tmp_s", [128, D], F32, stk)
            A_c, sh_c = make_mod_tiles(stk, l, 1, 0, 1, norm1_g[l:l + 1, :], tmp_g, tmp_s, "c")
            A_x, sh_x = make_mod_tiles(stk, l, 0, 0, 1, norm1_g[l:l + 1, :], tmp_g, tmp_s, "x")
            xrot = Rot([p.sb("xt%d" % i, [128, D], F32, stk) for i in range(2)])
            hrot = Rot([p.sb("hb%d" % i, [128, D], BF16, stk) for i in range(2)])
            junk = p.sb("junk", [128, D], BF16, stk)
            ssr = Rot([p.sb("ss%d" % i, [128, 1], F32, stk) for i in range(2)])
            for i in range(NT):
                xt = xrot.next()
                src, rd = x_src(l, i)
                p.dma("sp", xt[:], src, rd, [xt], xt)
                hb = hrot.next()
                A, sh = (A_c, sh_c) if i < 2 else (A_x, sh_x)
                sst = ssr.next()
                norm_mod_tile(xt, A, sh, hb, sst, junk)
                if debug and i == 2 and l == 0:
                    p.dma("sp", dbg_ss, sst[:], [sst], [R["yout"]], sst)
                    p.dma("sp", dbg_h, hb[:], [hb], [R["yout"]], hb)
                    p.dma("sp", dbg_x, xt[:], [xt], [R["yout"]], xt)
                transpose_to(hb, hT, hT_t[i], i * 128)
            if stage == -1:
                p.finish()
                return "stop"
        p.barrier()
        with p.scope() as stk:
            w2v = w2[l].rearrange("(kc p) n -> p kc n", p=128)
            wrot = Rot([p.sb("wi%d" % i, [128, 16, 512], BF16, stk) for i in range(2)])
            rbrot = Rot([p.sb("rb%d" % i, [128, 2308], F32, stk) for i in range(2)])
            u = p.sb("u", [128, 2308], F32, stk)
            ux1 = p.sb("ux1", [128, 2308], F32, stk)
            strot = Rot([p.sb("stg%d" % i, [128, NTOK], BF16, stk) for i in range(2)])
            tmrot = Rot([p.sb("tms%d" % i, [128, 512], BF16, stk) for i in range(3)])
            tgrot = Rot([p.sb("tgs%d" % i, [128, 16], F32, stk) for i in range(2)])
            cw = p.sb("cw", [128, 28, 3], F32, stk)
            ropec = p.sb("ropec", [128, L], F32, stk)
            ropes = p.sb("ropes", [128, L], F32, stk)
            p.dma("sp", cw[:], convw[l], [], [cw], cw)
            p.dma("sp", ropec[:], rope_in[0], [], [ropec], ropec)
            p.dma("sp", ropes[:], rope_in[1], [], [ropes], ropes)
            for rb in rbrot.tiles:
                p.memset("dve", rb[:], 0.0, [rb])
            p.memset("dve", u[:], 0.0, [u])
            psrot = Rot([PS[0], PS[1], PS[2], PS[3]])

            def conv3(rb, cc, dst):
                p.ts("dve", dst[:, 1:2307], rb[:, 1:2307], cw[:, cc, 1:2], None, ALU.mult, reads=[rb, cw], writes=[dst])
                p.stt("dve", dst[:, 1:2307], rb[:, 0:2306], cw[:, cc, 0:1], dst[:, 1:2307], ALU.mult, ALU.add, [rb, cw, dst], [dst])
                p.stt("dve", dst[:, 1:2307], rb[:, 2:2308], cw[:, cc, 2:3], dst[:, 1:2307], ALU.mult, ALU.add, [rb, cw, dst], [dst])

            def to_stage(e, st, srcbuf, with_ctx, func=None):
                parts = ([(0, 256, 1)] if with_ctx else []) + [(256, 2048, 259)]
                for (d0, n, s0) in parts:
                    if func is None:
                        p.copy(e, st[:, d0:d0 + n], srcbuf[:, s0:s0 + n], [srcbuf], [st])
                    else:
                        p.act(st[:, d0:d0 + n], srcbuf[:, s0:s0 + n], func, [srcbuf], [st])

            def store_fm(dst_ap, dres, st, with_ctx):
                if with_ctx:
                    p.dma("sp", dst_ap, st[:], [st], [dres], st)
                else:
                    p.dma("sp", dst_ap[:, 256:NTOK], st[:, 256:NTOK], [st], [dres], st)

            wt = None
            rbA = None
            for fc in range(NFM):
                if fc % 4 == 0:
                    wt = wrot.next()
                    p.dma("pool", wt[:], w2v[:, :, fc * 128:fc * 128 + 512], [], [wt], wt)
                with_ctx = not (l == 1 and fc < 12)
                rb = rbrot.next()
                for (t0, nt) in GROUPS:
                    if t0 == 0 and not with_ctx:
                        continue
                    N = nt * 128
                    ps = psrot.next()
                    for kc in range(16):
                        p.mm(ps[:, 0:N], wt[:, kc, (fc % 4) * 128:(fc % 4) * 128 + 128], hT[:, kc, t0 * 128:t0 * 128 + N],
                             kc == 0, kc == 15, [wt] + hT_t[t0:t0 + nt], [ps])
                    o0 = 1 if t0 == 0 else 3 + t0 * 128
                    p.copy("act", rb[:, o0:o0 + N], ps[:, 0:N], [ps], [rb])
                if fc < 12:
                    j, kind = fc // 3, fc % 3
                    if kind == 0:
                        conv3(rb, fc, u)
                        st = strot.next()
                        to_stage("dve", st, u, with_ctx)
                        store_fm(fmq[fc], R["fmq"], st, with_ctx)
                    elif kind == 1:
                        conv3(rb, fc, ux1)
                    else:
                        conv3(rb, fc, u)
                        p.tt("dve", u[:, 1:2307], u[:, 1:2307], ux1[:, 1:2307], ALU.mult, [u, ux1], [u])
                        st = strot.next()
                        to_stage("dve", st, u, with_ctx)
                        store_fm(hyz[j], R["hyz"], st, with_ctx)
                elif fc < 28:
                    conv3(rb, fc, u)
                    st = strot.next()
                    to_stage("act", st, u, True, AF.Silu)
                    store_fm(fmq[fc], R["fmq"], st, True)
                else:
                    if fc % 2 == 0:
                        rbA = rb
                    else:
                        rbB = rb
                        la = slice(259, 2307)
                        p.tt("dve", u[:, la], rbA[:, la], ropec[:], ALU.mult, [rbA, ropec], [u])
                        p.tt("dve", ux1[:, la], rbB[:, la], ropes[:], ALU.mult, [rbB, ropes], [ux1])
                        p.tt("dve", u[:, la], u[:, la], ux1[:, la], ALU.subtract, [u, ux1], [u])
                        p.copy("dve", u[:, 1:257], rbA[:, 1:257], [rbA], [u])
                        st = strot.next()
                        to_stage("dve", st, u, True)
                        store_fm(fmq[fc - 1], R["fmq"], st, True)
                        p.tt("dve", u[:, la], rbA[:, la], ropes[:], ALU.mult, [rbA, ropes], [u])
                        p.tt("dve", ux1[:, la], rbB[:, la], ropec[:], ALU.mult, [rbB, ropec], [ux1])
                        p.tt("dve", u[:, la], u[:, la], ux1[:, la], ALU.add, [u, ux1], [u])
                        p.copy("dve", u[:, 1:257], rbB[:, 1:257], [rbB], [u])
                        st = strot.next()
                        to_stage("dve", st, u, True)
                        store_fm(fmq[fc], R["fmq"], st, True)
            for (off, wdt) in TM_GROUPS:
                wt = wrot.next()
                p.dma("pool", wt[:, :, 0:wdt], w2v[:, :, NFM * 128 + off:NFM * 128 + off + wdt], [], [wt], wt)
                for i in range(NT):
                    ps = psrot.next()
                    for kc in range(16):
                        p.mm(ps[:, 0:wdt], hT[:, kc, i * 128:(i + 1) * 128], wt[:, kc, 0:wdt], kc == 0, kc == 15,
                             [wt, hT_t[i]], [ps])
                    rows = slice(i * 128, (i + 1) * 128)
                    if off >= 3328:
                        tg = tgrot.next()
                        p.copy("act", tg[:], ps[:, 0:16], [ps], [tg])
                        p.dma("sp", tgate[rows, :], tg[:], [tg], [R["tgate"]], tg)
                        continue
                    st = tmrot.next()
                    if off < 1536:
                        p.copy("act", st[:, 0:wdt], ps[:, 0:wdt], [ps], [st])
                        p.dma("sp", tmv[rows, off:off + wdt], st[:, 0:wdt], [st], [R["tmv"]], st)
                    elif off < 2560:
                        p.act(st[:, 0:wdt], ps[:, 0:wdt], AF.Sigmoid, [ps], [st])
                        p.dma("sp", tmo[rows, off - 1536:off - 1536 + wdt], st[:, 0:wdt], [st], [R["tmo"]], st)
                    else:
                        p.act(st[:, 0:wdt], ps[:, 0:wdt], AF.Silu, [ps], [st])
                        p.dma("sp", tmg[rows, off - 2560:off - 2560 + wdt], st[:, 0:wdt], [st], [R["tmg"]], st)
        p.barrier()

    ytm = bufA.t[:, 0:NT * 1536].rearrange("p (c n) -> p c n", c=NT)
    yhT = bufA.t[:, NT * 1536:16 * NTOK].rearrange("p (j t) -> p j t", j=4)
    ytm_t = [p.res("ytm%d" % i) for i in range(NT)]
    yh_res = p.res("yhT")

    def mixers(l):
        with p.scope() as mstk:
            masks = [[p.sb("mask%d%d" % (d, j), [128, 512], F32, mstk) for j in range(4)] for d in range(2)]
            tris = [p.sb("tri%d" % d, [128, 128], F32, mstk) for d in range(2)]
            vpos = p.sb("vpos", [128, 2, NT], F32, mstk)
            hmask = p.sb("hmask", [128, 4], F32, mstk)
            for d in range(2):
                p.dma("sp", tris[d][:], tri_in[d], [], [tris[d]], tris[d])
                for j in range(4):
                    p.dma("sp", masks[d][j][:], mask_in[d, j], [], [masks[d][j]], masks[d][j])
            p.dma("sp", vpos[:], vpos_in, [], [vpos], vpos)
            p.dma("sp", hmask[:], hmask_in, [], [hmask], hmask)
            Bt = p.sb("Bt", [128, 2, NT, 4], F32, mstk)
            at = p.sb("at", [128, 2, NT, 4], F32, mstk)
            acc_sets = [[(PS[0], 0), (PS[1], 0), (PS[2], 0), (PS[3], 0)]] * 2
            acc_res = [[PS[0], PS[1], PS[2], PS[3]]] * 2
            acc_i = [0]
            psS = Rot([PS[4], PS[5], PS[7]])
            psB = PS[6]
            bbrot = Rot([p.sb("bbc%d" % i, [128, 512], F32, mstk) for i in range(2)])
            dgrot = Rot([p.sb("dg%d" % i, [128, 512], F32, mstk) for i in range(1)])
            tmrot2 = Rot([p.sb("tmpm%d" % i, [128, 512], F32, mstk) for i in range(4)])
            dtrot = Rot([p.sb("dt%d" % i, [128, 512], BF16, mstk) for i in range(4)])
            smrot = Rot([p.sb("sm%d" % i, [128, 512], BF16, mstk) for i in range(4)])
            hsum = p.sb("hsum", [128, 4, DH], F32, mstk)
            hrrot = Rot([p.sb("hraw%d" % i, [128, 4, DH + 1], F32, mstk) for i in range(2)])
            junk2 = p.sb("junk2", [128, DH], BF16, mstk)
            strot = Rot([p.sb("st8_%d" % i, [128, 8], F32, mstk) for i in range(4)])
            qkrot = Rot([p.sb("qk%d" % i, [128, NTOK], BF16, mstk) for i in range(8)])

            def attn_core(kind, h, qlo, qhi, klo, khi, vaug, ncol, scale, finalize):
                units = [(t0, w, d) for (t0, w) in GROUPS if not (t0 == 0 and l == 1) for d in range(2)]

                def prologue(u):
                    t0, w, d = u
                    wn = w * 128
                    dg = dgrot.next()
                    for j in range(w):
                        p.ts("dve", dg[:, j * 128:(j + 1) * 128], ident_f[:], Bt[:, d, t0 + j, h:h + 1], None, ALU.mult,
                             reads=[ident_f, Bt], writes=[dg])
                    p.mm(psB[:, 0:wn], ones_f[:], dg[:, 0:wn], True, True, [ones_f, dg], [psB])
                    bb = bbrot.next()
                    p.copy("act", bb[:, 0:wn], psB[:, 0:wn], [psB], [bb])
                    tms = []
                    for j in range(w):
                        tm = tmrot2.next()
                        p.tt("dve", tm[:, 0:wn], bb[:, 0:wn], masks[d][j][:, 0:wn], ALU.add, [bb, masks[d][j]], [tm])
                        tms.append(tm)
                    return bb, tms

                def run_steps(u, bb, tms):
                    t0, w, d = u
                    wn = w * 128
                    if d == 0:
                        steps = [(c, None) for c in range(t0)]
                    elif t0 == 0:
                        steps = []
                    else:
                        steps = [(0, None), (1, None)] + [(c, None) for c in range(t0 + w, NT)]
                    steps += [(t0 + j, j) for j in range(w)]
                    nst = len(steps)
                    accv = [PS[j].t for j in range(4)]
                    accr = [PS[j] for j in range(4)]

                    def stage1(si):
                        c, mj = steps[si]
                        src = bb if mj is None else tms[mj]
                        dt = dtrot.next()
                        p.act(dt[:, 0:wn], src[:, 0:wn], AF.Exp, [src, at], [dt], bias=at[:, d, c, h:h + 1])
                        ps = psS.next()
                        p.mm(ps[:, 0:wn], klo[:, c * 128:(c + 1) * 128], qlo[:, t0 * 128:t0 * 128 + wn], True, False, [klo, qlo], [ps])
                        p.mm(ps[:, 0:wn], khi[:, c * 128:(c + 1) * 128], qhi[:, t0 * 128:t0 * 128 + wn], False, True, [khi, qhi], [ps])
                        sm = smrot.next()
                        p.stt("dve", sm[:, 0:wn], ps[:, 0:wn], scale, dt[:, 0:wn], ALU.mult, ALU.mult, [ps, dt], [sm])
                        return sm

                    def stage2(si, sm):
                        c = steps[si][0]
                        for j in range(w):
                            p.mm(accv[j][:, 0:ncol], sm[:, j * 128:(j + 1) * 128], vaug[:, c, 0:ncol], si == 0, si == nst - 1,
                                 [sm, vaug], [accr[j]])
                    pend = []
                    for si in range(nst):
                        pend.append((si, stage1(si)))
                        if len(pend) > 2:
                            stage2(*pend.pop(0))
                    while pend:
                        stage2(*pend.pop(0))

                def epilogue(u):
                    t0, w, d = u
                    accv = [PS[j].t for j in range(4)]
                    accr = [PS[j] for j in range(4)]
                    hr = hrrot.next()
                    for j in range(w):
                        p.copy("act", hr[:, j, 0:ncol], accv[j][:, 0:ncol], [accr[j]], [hr])
                    if kind == "ml":
                        s8 = strot.next()
                        p.stt("dve", s8[:, 4:4 + w], hr[:, 0:w, DH], -1.0, hr[:, 0:w, DH], ALU.mult, ALU.max, [hr], [s8])
                        p.ts("dve", s8[:, 4:4 + w], s8[:, 4:4 + w], 1.0, None, ALU.max, reads=[s8], writes=[s8])
                        p.op("dve", lambda hh, o=s8[:, 0:w], i_=s8[:, 4:4 + w]: hh.reciprocal(o, i_), [s8], [s8])
                        for j in range(w):
                            if d == 0:
                                p.ts("dve", hsum[:, j, :], hr[:, j, 0:DH], s8[:, j:j + 1], None, ALU.mult, reads=[hr, s8], writes=[hsum])
                            else:
                                p.stt("dve", hsum[:, j, :], hr[:, j, 0:DH], s8[:, j:j + 1], hsum[:, j, :], ALU.mult, ALU.add,
                                      [hr, s8, hsum], [hsum])
                    else:
                        if d == 0:
                            p.copy("dve", hsum[:, 0:w, :], hr[:, 0:w, 0:DH], [hr], [hsum])
                        else:
                            p.tt("dve", hsum[:, 0:w, :], hr[:, 0:w, 0:DH], hsum[:, 0:w, :], ALU.add, [hr, hsum], [hsum])
                    if d == 1:
                        s8 = strot.next()
                        p.memset("dve", s8[:, 0:4], 0.0, [s8])
                        for j in range(w):
                            p.act(junk2[:], hsum[:, j, :], AF.Square, [hsum], [junk2, s8], accum_out=s8[:, j:j + 1])
                        rstd_from_sumsq(s8[:, 0:w], s8[:, 0:w], DH, [s8])
                        for j in range(w):
                            finalize(t0 + j, j, s8[:, j:j + 1])

                nxt = prologue(units[0])
                for ui, u in enumerate(units):
                    cur = nxt
                    run_steps(u, *cur)
                    if ui + 1 < len(units):
                        nxt = prologue(units[ui + 1])
                    epilogue(u)

            with p.scope() as stk:
                gt = p.sb("gt", [128, NT, 16], F32, stk)
                gbb = p.sb("gbb", [128, NT, 16], F32, stk)
                lf = p.sb("lf", [128, 2, NT, 4], F32, stk)
                it = p.sb("it", [128, 2, NT, 4], F32, stk)
                ngb = p.sb("ngb", [128, MLW], F32, stk)
                osig = p.sb("osig", [128, NT, MLW], BF16, stk)
                vrot = Rot([p.sb("vaug%d" % i, [128, NT, DH + 1], BF16, stk) for i in range(2)])
                p.dma("sp", gt[:], tgate.rearrange("(c p) n -> p c n", p=128), [R["tgate"]], [gt], gt)
                p.dma("sp", gbb[:], gateb_t[l:l + 1, :].partition_broadcast(128), [], [gbb], gbb)
                p.dma("sp", ngb[:], ml_norm_g[l:l + 1, :].partition_broadcast(128), [], [ngb], ngb)
                p.dma("sp", osig[:], tmo.rearrange("(c p) n -> p c n", p=128)[:, :, 0:MLW], [R["tmo"]], [osig], osig)
                p.tt("dve", gt[:], gt[:], gbb[:], ALU.add, [gt, gbb], [gt])
                for d in range(2):
                    p.copy("dve", it[:, d, :, :], gt[:, :, d * 8:d * 8 + 4], [gt], [it])
                    p.act(lf[:, d, :, :], gt[:, :, d * 8 + 4:d * 8 + 8], AF.Exp, [gt], [lf], scale=-1.0)
                p.act(lf[:], lf[:], AF.Ln, [lf], [lf], bias=one_t[:, 0:1])
                p.ts("dve", lf[:], lf[:], -1.0, None, ALU.mult, reads=[lf], writes=[lf])
                order = [list(range(NT)), [1, 0] + list(range(17, 1, -1))]
                for d in range(2):
                    for vi, c in enumerate(order[d]):
                        col = (d * NT + c) * 4
                        for pi, cp in enumerate(order[d][:vi]):
                            p.mm(psB[:, col:col + 4], ones_f[:], lf[:, d, cp, :], pi == 0, False, [ones_f, lf], [psB])
                        p.mm(psB[:, col:col + 4], tris[d][:], lf[:, d, c, :], vi == 0, True, [tris[d], lf], [psB])
                p.copy("act", Bt[:], psB[:, 0:2 * NT * 4].rearrange("p (d c h) -> p d c h", d=2, c=NT), [psB], [Bt])
                p.tt("dve", at[:], it[:], Bt[:], ALU.subtract, [it, Bt], [at])
                for tvt in vrot.tiles:
                    p.memset("dve", tvt[:], 1.0, [tvt])
                for h in range(4):
                    qlo, qhi, klo, khi = (qkrot.next() for _ in range(4))
                    for tl, idx in ((qlo, 12 + 2 * h), (qhi, 13 + 2 * h), (klo, 20 + 2 * h), (khi, 21 + 2 * h)):
                        p.dma("sp", tl[:], fmq[idx], [R["fmq"]], [tl], tl)
                    vaug = vrot.next()
                    p.dma("sp", vaug[:, :, 0:DH], tmv.rearrange("(c p) n -> p c n", p=128)[:, :, h * DH:(h + 1) * DH],
                          [R["tmv"]], [vaug], vaug)

                    def fin_ml(ti, j, s8, h=h):
                        p.stt("dve", hsum[:, j, :], hsum[:, j, :], s8, ngb[:, h * DH:(h + 1) * DH], ALU.mult, ALU.mult,
                              [hsum, ngb] + strot.tiles, [hsum])
                        p.tt("dve", ytm[:, ti, h * DH:(h + 1) * DH], hsum[:, j, :], osig[:, ti, h * DH:(h + 1) * DH], ALU.mult,
                             [hsum, osig], [ytm_t[ti]])
                    attn_core("ml", h, qlo, qhi, klo, khi, vaug, DH + 1, DH ** -0.5, fin_ml)
            p.barrier()
            if stage <= 2:
                return
            with p.scope() as stk:
                lg = p.sb("lg", [128, 8], F32, stk)
                gsl = p.sb("gsl", [128, NT, RTW], BF16, stk)
                vrot = Rot([p.sb("vrt%d" % i, [128, NT, DH], BF16, stk) for i in range(2)])
                rqk = [None] * 4
                p.dma("sp", lg[:], rt_ld[l:l + 1, :].partition_broadcast(128), [], [lg], lg)
                p.dma("sp", gsl[:], tmg.rearrange("(c p) n -> p c n", p=128), [R["tmg"]], [gsl], gsl)
                p.act(lg[:], lg[:], AF.Exp, [lg], [lg])
                p.ts("dve", lg[:], lg[:], -1.0, None, ALU.mult, reads=[lg], writes=[lg])
                for d in range(2):
                    for h in range(4):
                        p.ts("dve", Bt[:, d, :, h], vpos[:, d, :], lg[:, d * 4 + h:d * 4 + h + 1], None, ALU.mult,
                             reads=[vpos, lg], writes=[Bt])
                p.ts("dve", at[:], Bt[:], -1.0, None, ALU.mult, reads=[Bt], writes=[at])
                for h in range(4):
                    klo, khi = qkrot.next(), qkrot.next()
                    if h % 2 == 0:
                        rqk = [qkrot.next() for _ in range(4)]
                        for i, idx in enumerate((28 + h, 29 + h, 32 + h, 33 + h)):
                            p.dma("sp", rqk[i][:], fmq[idx], [R["fmq"]], [rqk[i]], rqk[i])
                    qA, qB, kA, kB = rqk
                    p.ts("dve", klo[:], kA[:], hmask[:, h:h + 1], None, ALU.mult, reads=[kA, hmask], writes=[klo])
                    p.ts("dve", khi[:], kB[:], hmask[:, h:h + 1], None, ALU.mult, reads=[kB, hmask], writes=[khi])
                    vt = vrot.next()
                    p.dma("sp", vt[:], tmv.rearrange("(c p) n -> p c n", p=128)[:, :, MLW + h * DH:MLW + (h + 1) * DH],
                          [R["tmv"]], [vt], vt)

                    def fin_rt(ti, j, s8, h=h):
                        p.stt("dve", ytm[:, ti, MLW + h * DH:MLW + (h + 1) * DH], hsum[:, j, :], s8,
                              gsl[:, ti, h * DH:(h + 1) * DH], ALU.mult, ALU.mult, [hsum, gsl] + strot.tiles, [ytm_t[ti]])
                    attn_core("rt", h, qA, qB, klo, khi, vt, DH, 96 ** -0.5, fin_rt)
            p.barrier()

    negpi_t = p.sb("negpi_t", [128, 1], F32)
    p.memset("dve", negpi_t[:], -math.pi, [negpi_t])

    def hyena(l, seq):
        n = L if seq == "x" else LC
        ndc = n // 128
        col0 = 256 if seq == "x" else 0
        dft = dftx_in if seq == "x" else dftc_in
        z_in = zx_in if seq == "x" else zc_in
        win_in = winx_in if seq == "x" else winc_in
        hw_in = hwx_in if seq == "x" else hwc_in
        tw = min(512, n)
        with p.scope() as ostk:
            Pre = p.sb("Pre", [128, ndc, 512], BF16, ostk)
            nPim = p.sb("nPim", [128, ndc, 512], BF16, ostk)
            hbias = p.sb("hbias", [128, 4], F32, ostk)
            p.dma("sp", hbias[:], hy_bias[l], [], [hbias], hbias)
            with p.scope() as bstk:
                hsum_b = p.sb("hsum_b", [128, ndc, 512], BF16, bstk)
                hdif_b = p.sb("hdif_b", [128, ndc, 512], BF16, bstk)
                invn = p.sb("invn", [128, 512], F32, bstk)
                hwt = p.sb("hwt", [128, ndc, 2], F32, bstk)
                hwn = p.sb("hwn", [128, ndc], F32, bstk)
                p.dma("sp", hwt[:], hw_in, [], [hwt], hwt)
                p.ts("dve", hwn[:], hwt[:, :, 0], -1.0, None, ALU.mult, reads=[hwt], writes=[hwn])
                with p.scope() as stk:
                    w1 = p.sb("hw1", [33, 64], F32, stk)
                    w2_ = p.sb("hw2", [64, 64], F32, stk)
                    w3 = p.sb("hw3", [64, 1024], F32, stk)
                    hv = p.sb("hv", [64, 3], F32, stk)
                    frb = p.sb("frb", [64, 2], F32, stk)
                    zT = p.sb("zT", [33, n], F32, stk)
                    hd1 = p.sb("hd1", [64, n], F32, stk)
                    hd2 = p.sb("hd2", [64, n], F32, stk)
                    arg = p.sb("arg", [64, 512], F32, stk)
                    argi = p.sb("argi", [64, 512], I32, stk)
                    argf = p.sb("argf", [64, 512], F32, stk)
                    winr = Rot([p.sb("win%d" % i, [128, 512], F32, stk) for i in range(2)])
                    hfw = p.sb("hfw", [128, 512], F32, stk)
                    hbw = p.sb("hbw", [128, 512], F32, stk)
                    absr = Rot([p.sb("abs%d" % i, [128, 512], BF16, stk) for i in range(2)])
                    p.dma("sp", w1[:], hy_w1[l], [], [w1], w1)
                    p.dma("sp", w2_[:], hy_w2[l], [], [w2_], w2_)
                    p.dma("sp", w3[:], hy_w3[l], [], [w3], w3)
                    p.dma("sp", hv[:], hy_vec[l], [], [hv], hv)
                    p.dma("sp", zT[:], z_in, [], [zT], zT)
                    p.ts("dve", frb[:, 0:1], hv[:, 0:1], hv[:, 2:3], None, ALU.mult, reads=[hv], writes=[frb])
                    p.ts("dve", frb[:, 1:2], hv[:, 1:2], hv[:, 2:3], None, ALU.mult, reads=[hv], writes=[frb])

                    def mlp_layer(wt_, kdim, src, dst, bcol):
                        for c0 in range(0, n, 512):
                            cw_ = min(512, n - c0)
                            ps = PS[0]
                            p.mm(ps[0:64, 0:cw_], wt_[0:kdim, 0:64], src[0:kdim, c0:c0 + cw_], True, True, [wt_, src], [ps])
                            p.ts("dve", arg[:, 0:cw_], ps[0:64, 0:cw_], hv[:, 2:3], frb[:, bcol:bcol + 1], ALU.mult, op1=ALU.add,
                                 reads=[ps, hv, frb], writes=[arg])
                            p.ts("dve", arg[:, 0:cw_], arg[:, 0:cw_], 1.0 / (2 * math.pi), 64.5, ALU.mult, op1=ALU.add, reads=[arg], writes=[arg])
                            p.copy("dve", argi[:, 0:cw_], arg[:, 0:cw_], [arg], [argi])
                            p.copy("dve", argf[:, 0:cw_], argi[:, 0:cw_], [argi], [argf])
                            p.tt("dve", arg[:, 0:cw_], arg[:, 0:cw_], argf[:, 0:cw_], ALU.subtract, [arg, argf], [arg])
                            p.ts("dve", argf[:, 0:cw_], arg[:, 0:cw_], 0.0, None, ALU.is_lt, reads=[arg], writes=[argf])
                            p.tt("dve", arg[:, 0:cw_], arg[:, 0:cw_], argf[:, 0:cw_], ALU.add, [arg, argf], [arg])
                            p.act(dst[:, c0:c0 + cw_], arg[:, 0:cw_], AF.Sin, [arg], [dst], scale=2 * math.pi, bias=negpi_t[0:64, 0:1])
                    mlp_layer(w1, 33, zT, hd1, 0)
                    mlp_layer(w2_, 64, hd1, hd2, 1)
                    psL1 = PS[6]
                    for dc in range(ndc):
                        wn_ = winr.next()
                        p.dma("sp", wn_[:], win_in[dc * 128:(dc + 1) * 128, :], [], [wn_], wn_)
                        for half, dstw in ((0, hfw), (1, hbw)):
                            ps = PS[1 + half]
                            p.mm(ps[:, :], hd2[0:64, dc * 128:(dc + 1) * 128], w3[0:64, half * 512:(half + 1) * 512], True, True, [hd2, w3], [ps])
                            p.tt("dve", dstw[:], ps[:, :], wn_[:], ALU.mult, [ps, wn_], [dstw])
                        if dc == 0:
                            p.memset("dve", hbw[0:1, :], 0.0, [hbw])
                        p.tt("dve", hsum_b[:, dc, :], hfw[:], hbw[:], ALU.add, [hfw, hbw], [hsum_b])
                        p.tt("dve", hdif_b[:, dc, :], hfw[:], hbw[:], ALU.subtract, [hfw, hbw], [hdif_b])
                        for half, srcw in ((0, hfw), (1, hbw)):
                            ab = absr.next()
                            p.stt("dve", ab[:], srcw[:], -1.0, srcw[:], ALU.mult, ALU.max, [srcw], [ab])
                            p.mm(psL1[:, :], ones_b[:], ab[:], dc == 0 and half == 0, dc == ndc - 1 and half == 1, [ones_b, ab], [psL1])
                    p.op("dve", lambda hh: hh.reciprocal(invn[:], psL1[:, :]), [psL1], [invn])
                p.barrier()
                with p.scope() as stk:
                    zt = p.sb("zt", [128, ndc, 512], BF16, stk)
                    zjr = Rot([p.sb("zj%d" % i, [128, n], BF16, stk) for i in range(2)])
                    crot = Rot([p.sb("Cf%d" % i, [128, ndc, 128], BF16, stk) for i in range(2)])
                    srot = Rot([p.sb("Sf%d" % i, [128, ndc, 128], BF16, stk) for i in range(2)])
                    FKre = p.sb("FKre", [128, 512], F32, stk)
                    FKim = p.sb("FKim", [128, 512], F32, stk)
                    t1 = p.sb("t1", [128, 512], F32, stk)
                    t2 = p.sb("t2", [128, 512], F32, stk)
                    for j in range(4):
                        zj = zjr.next()
                        p.dma("sp", zj[:], hyz[j][:, col0:col0 + n], [R["hyz"]], [zj], zj)
                        for d0 in range(0, ndc, 4):
                            nb = min(4, ndc - d0)
                            for b in range(nb):
                                p.tr(ps_bf[:, b * 128:(b + 1) * 128], zj[:, (d0 + b) * 128:(d0 + b + 1) * 128], ident_b[:], [zj, ident_b], [PS[7]])
                            p.copy("act", zt[:, d0:d0 + nb, j * 128:(j + 1) * 128],
                                   ps_bf[:, 0:nb * 128].rearrange("p (k t) -> p k t", k=nb), [PS[7]], [zt])
                    dC = dft[0].rearrange("(dc p) f -> p dc f", p=128)
                    dS = dft[1].rearrange("(dc p) f -> p dc f", p=128)
                    for fc in range(ndc):
                        Cf, Sf = crot.next(), srot.next()
                        p.dma("pool", Cf[:], dC[:, :, fc * 128:(fc + 1) * 128], [], [Cf], Cf)
                        p.dma("pool", Sf[:], dS[:, :, fc * 128:(fc + 1) * 128], [], [Sf], Sf)
                        combos = [(Cf, hsum_b), (Sf, hsum_b), (Cf, hdif_b), (Sf, hdif_b), (Cf, zt), (Sf, zt)]
                        for ci, (tb, rh) in enumerate(combos):
                            for dc in range(ndc):
                                p.mm(PS[ci][:, :], tb[:, dc, :], rh[:, dc, :], dc == 0, dc == ndc - 1, [tb, rh], [PS[ci]])
                        A1, A2, A3, A4, Zc, Zs = PS[0], PS[1], PS[2], PS[3], PS[4], PS[5]
                        p.ts("dve", FKre[:], A1[:, :], hwt[:, fc, 0:1], None, ALU.mult, reads=[A1, hwt], writes=[FKre])
                        p.stt("dve", FKre[:], A2[:, :], hwt[:, fc, 1:2], FKre[:], ALU.mult, ALU.add, [A2, hwt, FKre], [FKre])
                        p.tt("dve", FKre[:], FKre[:], invn[:], ALU.mult, [FKre, invn], [FKre])
                        p.ts("dve", FKim[:], A3[:, :], hwt[:, fc, 1:2], None, ALU.mult, reads=[A3, hwt], writes=[FKim])
                        p.stt("dve", FKim[:], A4[:, :], hwn[:, fc:fc + 1], FKim[:], ALU.mult, ALU.add, [A4, hwn, FKim], [FKim])
                        p.tt("dve", FKim[:], FKim[:], invn[:], ALU.mult, [FKim, invn], [FKim])
                        p.tt("dve", t1[:], Zc[:, :], FKre[:], ALU.mult, [Zc, FKre], [t1])
                        p.tt("dve", t2[:], Zs[:, :], FKim[:], ALU.mult, [Zs, FKim], [t2])
                        p.tt("dve", Pre[:, fc, :], t1[:], t2[:], ALU.add, [t1, t2], [Pre])
                        p.tt("dve", t1[:], Zs[:, :], FKre[:], ALU.mult, [Zs, FKre], [t1])
                        p.tt("dve", t2[:], Zc[:, :], FKim[:], ALU.mult, [Zc, FKim], [t2])
                        p.tt("dve", nPim[:, fc, :], t1[:], t2[:], ALU.subtract, [t1, t2], [nPim])
                p.barrier()
            p.barrier()
            with p.scope() as stk:
                ctr = Rot([p.sb("Ct%d" % i, [128, ndc, tw], BF16, stk) for i in range(2)])
                str_ = Rot([p.sb("St%d" % i, [128, ndc, tw], BF16, stk) for i in range(2)])
                zr = Rot([p.sb("zti%d" % i, [128, tw], BF16, stk) for i in range(2)])
                xr = Rot([p.sb("x0i%d" % i, [128, tw], BF16, stk) for i in range(2)])
                t1r = Rot([p.sb("iv%d" % i, [128, tw], F32, stk) for i in range(2)])
                psr = Rot([PS[0], PS[1], PS[2], PS[3]])
                dCt = dft[0].rearrange("(fc p) t -> p fc t", p=128)
                dSt = dft[1].rearrange("(fc p) t -> p fc t", p=128)
                for tg in range(n // tw):
                    Ct, St = ctr.next(), str_.next()
                    p.dma("pool", Ct[:], dCt[:, :, tg * tw:(tg + 1) * tw], [], [Ct], Ct)
                    p.dma("pool", St[:], dSt[:, :, tg * tw:(tg + 1) * tw], [], [St], St)
                    tc0 = col0 + tg * tw
                    for j in range(4):
                        ps = psr.next()
                        for fc in range(ndc):
                            p.mm(ps[:, 0:tw], Pre[:, fc, j * 128:(j + 1) * 128], Ct[:, fc, :], fc == 0, False, [Pre, Ct], [ps])
                            p.mm(ps[:, 0:tw], nPim[:, fc, j * 128:(j + 1) * 128], St[:, fc, :], False, fc == ndc - 1, [nPim, St], [ps])
                        zi, xi, iv = zr.next(), xr.next(), t1r.next()
                        p.dma("sp", zi[:], hyz[j][:, tc0:tc0 + tw], [R["hyz"]], [zi], zi)
                        p.dma("sp", xi[:], fmq[3 * j][:, tc0:tc0 + tw], [R["fmq"]], [xi], xi)
                        p.act(iv[:], ps[:, 0:tw], AF.Copy, [ps], [iv], scale=1.0 / n)
                        p.stt("dve", iv[:], zi[:], hbias[:, j:j + 1], iv[:], ALU.mult, ALU.add, [zi, hbias, iv], [iv])
                        p.tt("dve", yhT[:, j, tc0:tc0 + tw], iv[:], xi[:], ALU.mult, [iv, xi], [yh_res])
            p.barrier()

    def fill_mod(A, sh, l, r, j_shift, j_scale, g_row):
        bcast_row(A, modv[l, r:r + 1, j_scale * D:(j_scale + 1) * D])
        bcast_row(sh, g_row)
        p.stt("dve", A[:], A[:], 1.0, sh[:], ALU.add, ALU.mult, [A, sh], [A])
        bcast_row(sh, modv[l, r:r + 1, j_shift * D:(j_shift + 1) * D])

    def outproj(l):
        with p.scope() as stk:
            wo = p.sb("wo", [128, 16, D], BF16, stk)
            wov = w_out[l].rearrange("(kc p) n -> p kc n", p=128)
            for cg in range(4):
                p.dma("pool", wo[:, :, cg * 512:(cg + 1) * 512], wov[:, :, cg * 512:(cg + 1) * 512], [], [wo], wo)
            g_x = p.sb("g_x", [128, D], F32, stk)
            bcast_row(g_x, modv[l, 0:1, 2 * D:3 * D])
            g_c = None
            if l == 0:
                g_c = p.sb("g_c", [128, D], F32, stk)
                bcast_row(g_c, modv[l, 1:2, 2 * D:3 * D])
            xrot = Rot([p.sb("xo%d" % i, [128, D], F32, stk) for i in range(2)])
            yrot = Rot([p.sb("yTt%d" % i, [128, 12, 128], BF16, stk) for i in range(2)])
            trot = Rot([p.sb("to%d" % i, [128, 512], F32, stk) for i in range(2)])
            psrot = Rot([PS[0], PS[1], PS[2], PS[3]])
            for i in (range(NT) if l == 0 else range(2, NT)):
                xt = xrot.next()
                src, rd = x_src(l, i)
                p.dma("sp", xt[:], src, rd, [xt], xt)
                yt = yrot.next()
                for b3 in range(3):
                    for b in range(4):
                        k = b3 * 4 + b
                        p.tr(ps_bf[:, b * 128:(b + 1) * 128], ytm[:, i, k * 128:(k + 1) * 128], ident_b[:], [ytm_t[i], ident_b], [PS[7]])
                    p.copy("act", yt[:, b3 * 4:(b3 + 1) * 4, :], ps_bf[:, 0:512].rearrange("p (k t) -> p k t", k=4), [PS[7]], [yt])
                gate = g_c if i < 2 else g_x
                for cg in range(4):
                    ps = psrot.next()
                    for k in range(16):
                        if k < 4:
                            lhsT, rd2 = yhT[:, k, i * 128:(i + 1) * 128], yh_res
                        else:
                            lhsT, rd2 = yt[:, k - 4, :], yt
                        p.mm(ps[:, :], lhsT, wo[:, k, cg * 512:(cg + 1) * 512], k == 0, k == 15, [rd2, wo], [ps])
                    tmp = trot.next()
                    p.tt("dve", tmp[:], ps[:, :], gate[:, cg * 512:(cg + 1) * 512], ALU.mult, [ps, gate], [tmp])
                    p.tt("dve", xt[:, cg * 512:(cg + 1) * 512], xt[:, cg * 512:(cg + 1) * 512], tmp[:], ALU.add, [xt, tmp], [xt])
                p.dma("sp", xs[i * 128:(i + 1) * 128, :], xt[:], [xt], [xs_t[i]], xt)

    hidden = bufA.t[:, 0:64 * 512].rearrange("p (j t) -> p j t", j=64)

    def ffn(l):
        final = (l == DEPTH - 1)
        groups = GROUPS if l == 0 else GROUPS[1:]
        with p.scope() as stk:
            A = p.sb("A2", [128, D], F32, stk)
            sh = p.sb("sh2", [128, D], F32, stk)
            gt = p.sb("g2", [128, D], F32, stk)
            xg_t = [p.sb("xg%d" % j, [128, D], F32, stk) for j in range(4)]
            h2T = p.sb("h2T", [128, 16, 512], BF16, stk)
            wrot = Rot([p.sb("wf%d" % i, [128, 16, 512], BF16, stk) for i in range(2)])
            scratch = p.sb("scr", [128, D], F32, stk)
            hb = p.sb("hb2", [128, D], BF16, stk)
            rrot = Rot([p.sb("rr%d" % i, [128, 512], BF16, stk) for i in range(2)])
            trot = Rot([p.sb("tf%d" % i, [128, 512], F32, stk) for i in range(2)])
            ssr = Rot([p.sb("ssf%d" % i, [128, 1], F32, stk) for i in range(2)])
            if final:
                fg = p.sb("fg", [128, D], F32, stk)
                p.dma("sp", fg[:], final_g.partition_broadcast(128), [], [fg], fg)
            w1v = w_ff1[l].rearrange("(kc p) n -> p kc n", p=128)
            ps1 = Rot([PS[4], PS[5]])
            cur_r = None
            for (t0, w) in groups:
                r = 1 if t0 == 0 else 0
                if r != cur_r:
                    fill_mod(A, sh, l, r, 3, 4, norm2_g[l:l + 1, :])
                    bcast_row(gt, modv[l, r:r + 1, 5 * D:6 * D])
                    cur_r = r
                N = w * 128
                for j in range(w):
                    i = t0 + j
                    p.dma("sp", xg_t[j][:], xs[i * 128:(i + 1) * 128, :], [xs_t[i]], [xg_t[j]], xg_t[j])
                    ss = ssr.next()
                    p.memset("dve", ss[:], 0.0, [ss])
                    p.act(hb[:], xg_t[j][:], AF.Square, [xg_t[j]], [hb, ss], accum_out=ss[:, 0:1])
                    rstd_from_sumsq(ss[:, 0:1], ss[:, 0:1], D, [ss])
                    p.stt("dve", scratch[:], xg_t[j][:], ss[:, 0:1], A[:], ALU.mult, ALU.mult, [xg_t[j], ss, A], [scratch])
                    p.tt("dve", hb[:], scratch[:], sh[:], ALU.add, [scratch, sh], [hb])
                    transpose_to(hb, h2T, h2T, j * 128)
                for wi in range(16):
                    wt = wrot.next()
                    p.dma("pool", wt[:], w1v[:, :, wi * 512:(wi + 1) * 512], [], [wt], wt)
                    for jj in range(4):
                        ps = ps1.next()
                        for k in range(16):
                            p.mm(ps[:, 0:N], wt[:, k, jj * 128:(jj + 1) * 128], h2T[:, k, 0:N], k == 0, k == 15, [wt, h2T], [ps])
                        rr = rrot.next()
                        p.act(rr[:, 0:N], ps[:, 0:N], AF.Relu, [ps], [rr])
                        p.tt("dve", hidden[:, wi * 4 + jj, 0:N], rr[:, 0:N], rr[:, 0:N], ALU.mult, [rr], [bufA])
                for cg in range(4):
                    for kq in range(4):
                        wt = wrot.next()
                        p.dma("pool", wt[:], w_ff2[l, kq * 2048:(kq + 1) * 2048, cg * 512:(cg + 1) * 512].rearrange("(kc p) n -> p kc n", p=128),
                              [], [wt], wt)
                        for j in range(w):
                            for k in range(16):
                                p.mm(PS[j][:, :], hidden[:, kq * 16 + k, j * 128:(j + 1) * 128], wt[:, k, :],
                                     kq == 0 and k == 0, kq == 3 and k == 15, [bufA, wt], [PS[j]])
                    for j in range(w):
                        tmp = trot.next()
                        p.tt("dve", tmp[:], PS[j][:, :], gt[:, cg * 512:(cg + 1) * 512], ALU.mult, [PS[j], gt], [tmp])
                        p.tt("dve", xg_t[j][:, cg * 512:(cg + 1) * 512], xg_t[j][:, cg * 512:(cg + 1) * 512], tmp[:], ALU.add,
                             [xg_t[j], tmp], [xg_t[j]])
                for j in range(w):
                    i = t0 + j
                    if not final:
                        p.dma("sp", xs[i * 128:(i + 1) * 128, :], xg_t[j][:], [xg_t[j]], [xs_t[i]], xg_t[j])
                    else:
                        ss = ssr.next()
                        p.memset("dve", ss[:], 0.0, [ss])
                        p.act(hb[:], xg_t[j][:], AF.Square, [xg_t[j]], [hb, ss], accum_out=ss[:, 0:1])
                        rstd_from_sumsq(ss[:, 0:1], ss[:, 0:1], D, [ss])
                        p.stt("dve", scratch[:], xg_t[j][:], ss[:, 0:1], fg[:], ALU.mult, ALU.mult, [xg_t[j], ss, fg], [scratch])
                        p.dma("sp", yout[(i - 2) * 128:(i - 1) * 128, :], scratch[:], [scratch], [R["yout"]], scratch)

    if debug:
        dbg_ss = nc.dram_tensor("dbg_ss", [128, 1], F32, kind="ExternalOutput").ap()
        dbg_h = nc.dram_tensor("dbg_h", [128, D], BF16, kind="ExternalOutput").ap()
        dbg_x = nc.dram_tensor("dbg_x", [128, D], F32, kind="ExternalOutput").ap()
    if layer_front(0) == "stop":
        return nc, p
    if stage <= 1:
        p.finish()
        return nc, p
    mixers(0)
    if stage >= 4:
        hyena(0, "x")
        hyena(0, "c")
    if debug:
        dbg_yh = nc.dram_tensor("dbg_yh", [128, 4 * NTOK], BF16, kind="ExternalOutput").ap()
        p.dma("sp", dbg_yh, bufA.t[:, NT * 1536:16 * NTOK], [yh_res], [R["yout"]], bufA)
        dbg_ytm = nc.dram_tensor("dbg_ytm", [128, NT * 1536], BF16, kind="ExternalOutput").ap()
        p.dma("sp", dbg_ytm, bufA.t[:, 0:NT * 1536], ytm_t, [R["yout"]], bufA)
    if stage <= 4:
        p.finish()
        return nc, p
    outproj(0)
    if stage <= 5:
        p.finish()
        return nc, p
    ffn(0)
    if stage <= 6:
        p.finish()
        return nc, p
    layer_front(1)
    mixers(1)
    hyena(1, "x")
    outproj(1)
    ffn(1)
    p.finish()
    return nc, p


_CONST_CACHE = {}


def const_inputs():
    if _CONST_CACHE:
        return _CONST_CACHE
    ident, tri_f, tri_b, mask, vpos, hmask = attn_consts()
    cosT, sinT = rope_tables()
    zx, winx = hyena_consts(L)
    zc, winc = hyena_consts(LC)
    cx, sx, cwx, swx = dft_tables(L)
    cc, sc, cwc, swc = dft_tables(LC)
    hwx = np.stack([cwx.reshape(16, 128).T, swx.reshape(16, 128).T], axis=-1)
    hwc = np.stack([cwc.reshape(2, 128).T, swc.reshape(2, 128).T], axis=-1)
    _CONST_CACHE.update(dict(
        ident=ident, tri=np.stack([tri_f, tri_b]), mask=mask, vpos=vpos, hmask=hmask,
        rope=np.stack([cosT, sinT]), zx=zx, zc=zc, winx=winx, winc=winc,
        dftx=np.stack([cx, sx]), dftc=np.stack([cc, sc]),
        hwx=np.ascontiguousarray(hwx, dtype=np.float32), hwc=np.ascontiguousarray(hwc, dtype=np.float32)))
    for k, v in _CONST_CACHE.items():
        _CONST_CACHE[k] = np.ascontiguousarray(v, dtype=np.float32)
    return _CONST_CACHE


def prep_inputs(x, c, ctx, c_ctx, norm1_g, norm2_g, w_mod, b_mod, w_in, hy_conv_w, hy_f_w1, hy_f_b1,
                hy_f_w2, hy_f_b2, hy_f_w3, hy_f_freq, hy_bias, ml_conv_w, ml_gate_b, ml_norm_g,
                rt_log_decay, w_out, w_ff1, w_ff2, final_g, cores=range(8)):
    f = lambda a: np.ascontiguousarray(np.asarray(a), dtype=np.float32)
    shared = dict(const_inputs())
    shared.update(
        norm1_g=f(norm1_g), norm2_g=f(norm2_g), final_g=f(final_g).reshape(1, D),
        w_mod=f(w_mod), b_mod=f(b_mod),
        w2=np.stack([build_w2(f(w_in[l])) for l in range(DEPTH)]),
        convw=np.stack([build_convw(f(hy_conv_w[l]), f(ml_conv_w[l])) for l in range(DEPTH)]),
        w_out=f(w_out), w_ff1=f(w_ff1), w_ff2=f(w_ff2),
        hy_w1=f(hy_f_w1), hy_w2=f(hy_f_w2), hy_w3=f(hy_f_w3),
        hy_vec=np.ascontiguousarray(np.stack([f(hy_f_b1), f(hy_f_b2), f(hy_f_freq)], axis=-1)),
        hy_bias=np.ascontiguousarray(f(hy_bias).reshape(DEPTH, 4, 128).transpose(0, 2, 1)),
        gateb_t=np.ascontiguousarray(np.tile(f(ml_gate_b), (1, NT))), ml_norm_g=f(ml_norm_g), rt_ld=f(rt_log_decay).reshape(DEPTH, 8),
    )
    x = f(x)
    ctx = f(ctx)
    c = f(c)
    c_ctx = f(c_ctx)
    maps = []
    for b in cores:
        m = dict(shared)
        m["xin"] = np.concatenate([ctx[b], x[b]], axis=0)
        m["c2"] = np.ascontiguousarray(np.stack([c[b].reshape(16, 128).T, c_ctx.reshape(16, 128).T], axis=-1))
        maps.append(m)
    return maps


_PROG_CACHE = {}


def kernel(**inputs):
    maps = prep_inputs(**inputs)
    nc, p = build_program()
    res = run_bass_kernel_spmd(nc, maps, core_ids=list(range(8)))
    return np.stack([np.asarray(r["yout"], dtype=np.float32) for r in res.results], axis=0)
```
